# Optimizing a Trainium2 kernel written in Bass

```python
import math
import jax, jax.numpy as jnp
from jax import lax
import numpy as np


D_MODEL = 1024
BATCH = 8
SEQ = 2048
DEPTH = 1
DEC_BATCH = 128
DEC_SEQ = 4
PAST_LEN = 16384
PAGE_SIZE = 128

D_MIX = D_MODEL
MIX_A = D_MIX // 2
A_HEADS = 4
A_DK = MIX_A // A_HEADS
A_DV = MIX_A // A_HEADS
MIX_B = D_MIX - MIX_A
CONV_W = 31
D_IN = 4 * MIX_A + 2 * MIX_B
D_FF = ((8 * D_MODEL + 3 * 256 - 1) // (3 * 256)) * 256
D_PLE = 256
CHUNK = 32
EPS = 1e-6

kernel_name = "hymba_hgrn2_conformer_conv_decode_step"


def _rmsnorm(x, g):
    xf = x.astype(jnp.float32)
    return xf * lax.rsqrt(jnp.mean(xf * xf, axis=-1, keepdims=True) + EPS) * g.astype(jnp.float32)


def _hgrn2_scan(q, k, v, logf, s0):
    B, T, H, DK = q.shape
    DV = v.shape[-1]
    C = math.gcd(T, CHUNK)
    n = T // C

    def to_chunks(a):
        return a.reshape(B, n, C, H, a.shape[-1]).transpose(1, 0, 3, 2, 4)

    qc, kc, vc, gc = to_chunks(q), to_chunks(k), to_chunks(v), to_chunks(logf)
    mask = jnp.tril(jnp.ones((C, C), dtype=bool))

    def step(s, inp):
        qi, ki, vi, gi = inp
        b = jnp.cumsum(gi, axis=2)
        b_last = b[:, :, -1:, :]
        q_dec = qi * jnp.exp(b)
        k_inv = ki * jnp.exp(-b)
        o_inter = jnp.einsum('bhtk,bhkv->bhtv', q_dec, s)
        scores = jnp.where(mask, jnp.einsum('bhtk,bhsk->bhts', q_dec, k_inv), 0.0)
        o_intra = jnp.einsum('bhts,bhsv->bhtv', scores, vi)
        s_new = jnp.exp(b_last[:, :, 0, :])[..., None] * s + jnp.einsum(
            'bhsk,bhsv->bhkv', ki * jnp.exp(b_last - b), vi)
        return s_new, o_inter + o_intra

    s_fin, o = lax.scan(step, s0, (qc, kc, vc, gc))
    o = o.transpose(1, 0, 3, 2, 4).reshape(B, T, H, DV)
    return o, s_fin


def _layer(h, p_l, s0, buf0, lb, norm_mix, w_in, hgrn_out_norm, conv_dw, conv_dw_bias,
           conv_ln_gain, conv_ln_bias, w_out, norm_ffn, w_ffn_gate, w_ffn_up, w_ffn_down,
           norm_ple, w_ple_gate, w_ple_proj):
    f32 = jnp.float32
    B, T, _ = h.shape
    xn = _rmsnorm(h, norm_mix)
    proj = xn @ w_in.astype(f32)
    q = proj[..., 0 * MIX_A:1 * MIX_A]
    fg = proj[..., 1 * MIX_A:2 * MIX_A]
    iv = proj[..., 2 * MIX_A:3 * MIX_A]
    og = proj[..., 3 * MIX_A:4 * MIX_A]
    ca = proj[..., 4 * MIX_A:4 * MIX_A + MIX_B]
    cg = proj[..., 4 * MIX_A + MIX_B:]

    def heads(a):
        return a.reshape(B, T, A_HEADS, -1)

    forget = lb + (1.0 - lb) * jax.nn.sigmoid(fg)
    o, s_new = _hgrn2_scan(heads(jax.nn.silu(q)), heads(1.0 - forget), heads(iv),
                           heads(jnp.log(forget)), s0.astype(f32))
    o = _rmsnorm(o, hgrn_out_norm) * jax.nn.silu(heads(og))
    o_a = o.reshape(B, T, MIX_A)

    u = ca * jax.nn.sigmoid(cg)
    u_full = jnp.concatenate([buf0.astype(f32), u], axis=1)
    z = lax.conv_general_dilated(u_full, conv_dw.astype(f32)[:, None, :], window_strides=(1,),
                                 padding='VALID', dimension_numbers=('NWC', 'WIO', 'NWC'),
                                 feature_group_count=MIX_B) + conv_dw_bias.astype(f32)
    mu = jnp.mean(z, axis=-1, keepdims=True)
    var = jnp.mean(jnp.square(z - mu), axis=-1, keepdims=True)
    z = (z - mu) * lax.rsqrt(var + EPS) * conv_ln_gain.astype(f32) + conv_ln_bias.astype(f32)
    o_b = jax.nn.silu(z)
    buf_new = u_full[:, T:, :]

    h = h + jnp.concatenate([o_a, o_b], axis=-1) @ w_out.astype(f32)

    xn2 = _rmsnorm(h, norm_ffn)
    h = h + (jax.nn.silu(xn2 @ w_ffn_gate.astype(f32)) * (xn2 @ w_ffn_up.astype(f32))) @ w_ffn_down.astype(f32)

    gate = jax.nn.sigmoid(_rmsnorm(h, norm_ple) @ w_ple_gate.astype(f32))
    h = h + gate * (p_l.astype(f32) @ w_ple_proj.astype(f32))
    return h, s_new, buf_new


def _trunk(x, p, st_hgrn, st_conv, norm_mix, w_in, lb_logits, hgrn_out_norm, conv_dw, conv_dw_bias,
           conv_ln_gain, conv_ln_bias, w_out, norm_ffn, w_ffn_gate, w_ffn_up, w_ffn_down,
           norm_ple, w_ple_gate, w_ple_proj, norm_final):
    h = x.astype(jnp.float32)
    lb_all = jnp.cumsum(jax.nn.softmax(lb_logits.astype(jnp.float32), axis=0), axis=0)
    new_s, new_c = [], []
    for l in range(DEPTH):
        h, s_l, c_l = _layer(h, p[l], st_hgrn[l], st_conv[l], lb_all[l], norm_mix[l], w_in[l],
                             hgrn_out_norm[l], conv_dw[l], conv_dw_bias[l], conv_ln_gain[l],
                             conv_ln_bias[l], w_out[l], norm_ffn[l], w_ffn_gate[l], w_ffn_up[l],
                             w_ffn_down[l], norm_ple[l], w_ple_gate[l], w_ple_proj[l])
        new_s.append(s_l)
        new_c.append(c_l)
    y = _rmsnorm(h, norm_final).astype(x.dtype)
    return y, jnp.stack(new_s, axis=0), jnp.stack(new_c, axis=0)


def setup_inputs(seed: int = 0) -> dict:
    key = jax.random.key(seed)
    ks = jax.random.split(key, 24)
    f32 = jnp.float32

    def nrm(k, shape, scale):
        return jax.random.normal(k, shape, f32) * scale

    return {
        "x_prompt": nrm(ks[0], (BATCH, SEQ, D_MODEL), 1.0),
        "x_sample": nrm(ks[1], (DEC_BATCH, DEC_SEQ, D_MODEL), 1.0),
        "p_prompt": nrm(ks[2], (DEPTH, BATCH, SEQ, D_PLE), 1.0),
        "p_sample": nrm(ks[3], (DEPTH, DEC_BATCH, DEC_SEQ, D_PLE), 1.0),
        "state_hgrn": nrm(ks[4], (DEPTH, DEC_BATCH, A_HEADS, A_DK, A_DV), 0.5),
        "state_conv": nrm(ks[5], (DEPTH, DEC_BATCH, CONV_W - 1, MIX_B), 0.5),
        "norm_mix": 1.0 + nrm(ks[6], (DEPTH, D_MODEL), 0.02),
        "w_in": nrm(ks[7], (DEPTH, D_MODEL, D_IN), D_MODEL ** -0.5),
        "lb_logits": nrm(ks[8], (DEPTH + 1, MIX_A), 0.1),
        "hgrn_out_norm": 1.0 + nrm(ks[9], (DEPTH, A_DV), 0.02),
        "conv_dw": nrm(ks[10], (DEPTH, CONV_W, MIX_B), CONV_W ** -0.5),
        "conv_dw_bias": nrm(ks[11], (DEPTH, MIX_B), 0.02),
        "conv_ln_gain": 1.0 + nrm(ks[12], (DEPTH, MIX_B), 0.02),
        "conv_ln_bias": nrm(ks[13], (DEPTH, MIX_B), 0.02),
        "w_out": nrm(ks[14], (DEPTH, D_MIX, D_MODEL), D_MIX ** -0.5),
        "norm_ffn": 1.0 + nrm(ks[15], (DEPTH, D_MODEL), 0.02),
        "w_ffn_gate": nrm(ks[16], (DEPTH, D_MODEL, D_FF), D_MODEL ** -0.5),
        "w_ffn_up": nrm(ks[17], (DEPTH, D_MODEL, D_FF), D_MODEL ** -0.5),
        "w_ffn_down": nrm(ks[18], (DEPTH, D_FF, D_MODEL), D_FF ** -0.5),
        "norm_ple": 1.0 + nrm(ks[19], (DEPTH, D_MODEL), 0.02),
        "w_ple_gate": nrm(ks[20], (DEPTH, D_MODEL, D_MODEL), D_MODEL ** -0.5),
        "w_ple_proj": nrm(ks[21], (DEPTH, D_PLE, D_MODEL), D_PLE ** -0.5),
        "norm_final": 1.0 + nrm(ks[22], (D_MODEL,), 0.02),
    }


def reference(x_prompt, x_sample, p_prompt, p_sample, state_hgrn, state_conv, norm_mix, w_in,
              lb_logits, hgrn_out_norm, conv_dw, conv_dw_bias, conv_ln_gain, conv_ln_bias, w_out,
              norm_ffn, w_ffn_gate, w_ffn_up, w_ffn_down, norm_ple, w_ple_gate, w_ple_proj,
              norm_final):
    B = x_prompt.shape[0]
    zero_s = jnp.zeros((DEPTH, B, A_HEADS, A_DK, A_DV), jnp.float32)
    zero_c = jnp.zeros((DEPTH, B, CONV_W - 1, MIX_B), jnp.float32)
    y_prompt, state_hgrn_prompt, state_conv_prompt = _trunk(
        x_prompt, p_prompt, zero_s, zero_c, norm_mix, w_in, lb_logits, hgrn_out_norm, conv_dw,
        conv_dw_bias, conv_ln_gain, conv_ln_bias, w_out, norm_ffn, w_ffn_gate, w_ffn_up,
        w_ffn_down, norm_ple, w_ple_gate, w_ple_proj, norm_final)
    y_sample, state_hgrn_sample, state_conv_sample = _trunk(
        x_sample, p_sample, state_hgrn, state_conv, norm_mix, w_in, lb_logits, hgrn_out_norm,
        conv_dw, conv_dw_bias, conv_ln_gain, conv_ln_bias, w_out, norm_ffn, w_ffn_gate, w_ffn_up,
        w_ffn_down, norm_ple, w_ple_gate, w_ple_proj, norm_final)
    return (y_prompt, y_sample, state_hgrn_prompt, state_conv_prompt, state_hgrn_sample, state_conv_sample)
```

```python
from contextlib import ExitStack
import numpy as np
import concourse.bass as bass
import concourse.mybir as mybir
from concourse.bass_utils import run_bass_kernel_spmd

F32, BF16 = mybir.dt.float32, mybir.dt.bfloat16
AF = mybir.ActivationFunctionType
ALU = mybir.AluOpType
ENGS = ("sync", "act", "dve", "pool", "pe")

D = 1024
DFF = 2816
NSEQ = 16
TS = 64
TP = 512
NPASS = 4
EPS = 1e-6
RING = 5
SLOT = 4096


class Prog:
    def __init__(self):
        self.ops = []
        self.last_w = {}
        self.readers = {}
        self.dma_keys = []

    def _add(self, eng, fn, reads, writes, dma_key=None):
        i = len(self.ops)
        deps = set()
        for r in reads:
            w = self.last_w.get(r)
            if w is not None:
                deps.add(w)
            if r == "phg" or r == "pkt" or (isinstance(r, tuple) and r[0] == "pg"):
                for r2 in self.readers.get(r, ()):
                    if self.ops[r2]["eng"] != eng:
                        deps.add(r2)
        for w_ in writes:
            w = self.last_w.get(w_)
            if w is not None:
                deps.add(w)
            for r in self.readers.get(w_, ()):
                deps.add(r)
        for r in reads:
            self.readers.setdefault(r, []).append(i)
        for w_ in writes:
            self.last_w[w_] = i
            self.readers[w_] = []
        self.ops.append(dict(eng=eng, fn=fn, deps=deps, dma_key=dma_key, idx=i))
        if dma_key is not None and dma_key not in self.dma_keys:
            self.dma_keys.append(dma_key)
        return i

    def op(self, eng, fn, reads=(), writes=()):
        return self._add(eng, fn, tuple(reads), tuple(writes))

    def dma(self, queue, out, in_, reads=(), writes=(), key=None):
        assert queue in ("sync", "act")
        return self._add(queue, lambda e: e.dma_start(out=out, in_=in_), tuple(reads), tuple(writes), key)

    def finalize(self):
        ops = self.ops
        need = [False] * len(ops)
        for o in ops:
            for d in o["deps"]:
                dd = ops[d]
                if dd["dma_key"] is None and dd["eng"] == "pe" and o["eng"] == "pe" and o["dma_key"] is None:
                    continue
                need[d] = True
        eng_cnt = {e: 0 for e in ENGS}
        dma_cnt = {k: 0 for k in self.dma_keys}
        sig = [None] * len(ops)
        for o in ops:
            i = o["idx"]
            if o["dma_key"] is not None:
                dma_cnt[o["dma_key"]] += 1
                sig[i] = ("dma", o["dma_key"], dma_cnt[o["dma_key"]] * 16)
            elif need[i]:
                eng_cnt[o["eng"]] += 1
                sig[i] = ("eng", o["eng"], eng_cnt[o["eng"]])
        cur = {k: 0 for k in self.dma_keys}
        for o in ops:
            waits = {}
            for d in o["deps"]:
                dd = ops[d]
                if dd["dma_key"] is not None:
                    k = ("dma", dd["dma_key"])
                    v = cur[dd["dma_key"]] * 16
                else:
                    if dd["eng"] == "pe" and o["eng"] == "pe" and o["dma_key"] is None:
                        continue
                    k = ("eng", dd["eng"])
                    v = sig[d][2]
                if waits.get(k, 0) < v:
                    waits[k] = v
            o["waits"] = waits
            if o["dma_key"] is not None:
                cur[o["dma_key"]] += 1
        self.final_dma = cur
        self.sig = sig

    def run(self, sems_eng, sems_dma, block):
        ops, sig = self.ops, self.sig
        per = {e: [o for o in ops if o["eng"] == e] for e in ENGS}

        def body(name):
            def f(eng):
                known = {}
                for o in per[name]:
                    for k, v in o["waits"].items():
                        if known.get(k, 0) >= v:
                            continue
                        known[k] = v
                        eng.wait_ge(sems_eng[k[1]] if k[0] == "eng" else sems_dma[k[1]], v)
                    ins = o["fn"](eng)
                    sg = sig[o["idx"]]
                    if sg is not None:
                        if sg[0] == "dma":
                            ins.then_inc(sems_dma[sg[1]], 16)
                        else:
                            ins.then_inc(sems_eng[sg[1]], 1)
                if name == "sync":
                    for k, v in self.final_dma.items():
                        if v > 0:
                            eng.wait_ge(sems_dma[k], v * 16)
            return f

        block.sync(body("sync"))
        block.scalar(body("act"))
        block.vector(body("dve"))
        block.gpsimd(body("pool"))
        block.tensor(body("pe"))


class Builder:
    def __init__(self, do_sample=True, stop=99, npass=NPASS):
        self.do_sample = do_sample
        self.stop = stop
        self.npass = npass
        self.nc = bass.Bass("TRN2", target_bir_lowering=False)
        self.P = Prog()
        self.es = ExitStack()
        self.bank_i = 0
        self.live = [False] * 6
        self.rr = {}

    def sb(self, name, shape, dt):
        return self.es.enter_context(self.nc.sbuf_tensor(name, shape, dt))

    def din(self, name, shape, dt=F32):
        return self.nc.dram_tensor(name, shape, dt, kind="ExternalInput").ap()

    def dout(self, name, shape, dt=F32):
        return self.nc.dram_tensor(name, shape, dt, kind="ExternalOutput").ap()

    def bank(self):
        for _ in range(6):
            i = self.bank_i % 6
            self.bank_i += 1
            if not self.live[i]:
                self.live[i] = True
                return self.pg[i], ("pg", i)
        raise RuntimeError("no free PSUM bank")

    def rel(self, bkey):
        assert self.live[bkey[1]]
        self.live[bkey[1]] = False

    def alt(self, key, choices):
        i = self.rr.get(key, 0)
        self.rr[key] = i + 1
        return choices[i % len(choices)]

    def act(self, out, in_, func, reads, writes, scale=1.0, bias=None):
        kw = dict(out=out, in_=in_, func=func, scale=scale)
        if bias is not None:
            kw["bias"] = bias
        self.P.op("act", lambda e: e.activation(**kw), reads, writes)

    def tt(self, eng, out, in0, in1, op, reads, writes):
        self.P.op(eng, lambda e: e.tensor_tensor(out=out, in0=in0, in1=in1, op=op), reads, writes)

    def ts(self, eng, out, in0, s1, s2, op0, op1, reads, writes):
        self.P.op(eng, lambda e: e.tensor_scalar(out=out, in0=in0, scalar1=s1, scalar2=s2, op0=op0, op1=op1),
                  reads, writes)

    def stt(self, out, in0, scalar, in1, op0, op1, reads, writes):
        self.P.op("dve", lambda e: e.scalar_tensor_tensor(out=out, in0=in0, scalar=scalar, in1=in1,
                                                          op0=op0, op1=op1), reads, writes)

    def cp(self, eng, out, in_, reads, writes):
        if eng == "act":
            self.act(out, in_, AF.Copy, reads, writes)
        else:
            self.P.op(eng, lambda e: e.tensor_copy(out=out, in_=in_), reads, writes)

    def mm(self, out, pairs, reads, writes, start=True, stop=True):
        n = len(pairs)

        def fn(e):
            ins = None
            for i, (l, r) in enumerate(pairs):
                ins = e.matmul(out, lhsT=l, rhs=r, start=(start and i == 0), stop=(stop and i == n - 1))
            return ins
        self.P.op("pe", fn, reads, writes)

    def tr(self, out, in_, ident, reads, writes):
        self.P.op("pe", lambda e: e.transpose(out=out, in_=in_, identity=ident), reads, writes)

    def build(self):
        nc, P = self.nc, self.P
        self.xp = self.din("xp", [2048, D]); self.xs = self.din("xs", [TS, D])
        self.pp = self.din("pp", [2048, 256]); self.ps_ = self.din("ps", [TS, 256])
        self.sh = self.din("sh", [NSEQ, 4, 128, 128]); self.sc = self.din("sc", [NSEQ, 30, 512])
        self.w_in = self.din("w_in", [D, 3072]); self.w_out = self.din("w_out", [D, D])
        self.wg = self.din("wg", [D, DFF]); self.wu = self.din("wu", [D, DFF]); self.wd = self.din("wd", [DFF, D])
        self.wpg = self.din("wpg", [D, D]); self.wpp = self.din("wpp", [256, D])
        self.vecs = self.din("vecs", [128, 64])
        self.convw = self.din("convw", [128, 4 * 31])
        self.yp = self.dout("yp", [2048, D]); self.ys = self.dout("ys", [TS, D])
        self.shp = self.dout("shp", [4, 128, 128]); self.scp = self.dout("scp", [30, 512])
        self.shs = self.dout("shs", [NSEQ, 4, 128, 128]); self.scs = self.dout("scs", [NSEQ, 30, 512])
        self.nblocks = 31
        self.wscr = nc.dram_tensor("wscr", [self.nblocks, 128, SLOT], BF16, kind="Internal").ap()

        sb = self.sb
        self.ident_f = sb("ident_f", [128, 128], F32); self.ident_b = sb("ident_b", [128, 128], BF16)
        self.ones_b = sb("ones_b", [128, 128], BF16); self.ones_f = sb("ones_f", [128, 128], F32)
        self.cmaskn = sb("cmaskn", [128, 128], BF16)
        self.smaskn = sb("smaskn", [64, 16, 4], BF16); self.seqmask = sb("seqmask", [64, 16], F32)
        self.smtmp = sb("smtmp", [64, 16, 4], F32)
        self.epsT = sb("epsT", [128, 1], F32)
        self.vec = sb("vec", [128, 64], F32)
        self.cw = sb("cw", [128, 4, 31], F32)
        self.fold_out = sb("fold_out", [128, 8], F32)
        self.lbv = sb("lbv", [128, 12], F32)
        self.hT = sb("hT", [128, 8, TP], F32)
        self.xn = sb("xn", [128, 8, TP], BF16)
        self.sq = sb("sq", [128, 3, TP], BF16)
        self.rs = sb("rs", [128, TP], F32); self.rt = sb("rt", [128, TP], F32)
        self.hid = sb("hid", [128, 22, TP], BF16)
        self.fb = sb("fb", [128, 16, TP], F32)
        self.cat = sb("cat", [128, 8, TP], BF16)
        self.ubuf = sb("ubuf", [128, 4, 30 + TP], BF16)
        self.pT = sb("pT", [128, 2, TP], BF16)
        self.xst = sb("xst", [128, 4, D], F32)
        self.yst = sb("yst", [128, 2, D], F32)
        self.pst = sb("pst", [128, 4, 256], F32)
        self.S = sb("S", [128, 4, 128], F32)
        self.tmpU = sb("tmpU", [128, 2, 128], F32)
        self.sm = sb("sm", [128, 4, 8], F32)
        self.sm2 = sb("sm2", [128, 4, 24], F32)
        self.ktok = sb("ktok", [128, 4, 512], BF16)
        self.scmb = sb("scmb", [128, 8, 128], BF16)
        self.Sbb = sb("Sbb", [128, 8, 128], BF16)
        self.wring = sb("wring", [128, RING, SLOT], BF16)
        self.wstage = sb("wstage", [128, 2, SLOT // 2], F32)
        self.wstage4 = self.wstage[:].rearrange("p a (b n) -> p (a b) n", n=SLOT // 4)
        self.cst = sb("cst", [128, 512], F32)
        self.ufs = sb("ufs", [128, 4, NSEQ, 34], BF16)
        if self.do_sample:
            self.S0b = self.xn[:, 0:4, :].rearrange("p a (b v) -> p (a b) v", v=128)
            self.vm = self.xn[:64, 4:8, :].rearrange("p a (b v) -> p (a b) v", v=128)
        self.pg = [self.es.enter_context(nc.psum_tensor(f"pg{i}", [128, 512], F32)) for i in range(6)]
        self.phg = self.es.enter_context(nc.psum_tensor("phg", [128, 512], F32))
        self.pkt = self.es.enter_context(nc.psum_tensor("pkt", [128, 1024], BF16))

        self.setup_consts()
        self.setup_stream()
        for pi in range(self.npass):
            self.run_pass(pi, TP, False)
        if self.do_sample:
            self.run_pass(NPASS, TS, True)

        P.finalize()
        sems_eng = {e: self.es.enter_context(nc.semaphore("se_" + e)) for e in ENGS}
        sems_dma = {k: self.es.enter_context(nc.semaphore("sd_%d" % i)) for i, k in enumerate(P.dma_keys)}
        block = self.es.enter_context(nc.Block())
        P.run(sems_eng, sems_dma, block)
        self.es.close()
        return nc

    def setup_consts(self):
        P = self.P
        P.dma("sync", self.vec[:], self.vecs, writes=["vec"], key="c_vec")
        P.dma("sync", self.cw[:].rearrange("p c j -> p (c j)"), self.convw, writes=["cw"], key="c_cw")
        ident_f, ident_b = self.ident_f, self.ident_b
        P.op("pool", lambda e: e.memset(ident_f[:], 1.0), writes=["ident_f"])
        P.op("pool", lambda e: e.affine_select(out=ident_f[:], in_=ident_f[:], pattern=[[-1, 128]],
                                               compare_op=ALU.is_equal, fill=0.0, base=0, channel_multiplier=1),
             reads=["ident_f"], writes=["ident_f"])
        self.cp("pool", ident_b[:], ident_f[:], ["ident_f"], ["ident_b"])
        ones_b, ones_f, epsT = self.ones_b, self.ones_f, self.epsT
        P.op("pool", lambda e: e.memset(ones_b[:], 1.0), writes=["ones_b"])
        P.op("pool", lambda e: e.memset(ones_f[:], 1.0), writes=["ones_f"])
        P.op("pool", lambda e: e.memset(epsT[:], EPS), writes=["epsT"])
        cm = self.cmaskn
        P.op("pool", lambda e: e.memset(cm[:], -1.0), writes=["cmaskn"])
        P.op("pool", lambda e: e.affine_select(out=cm[:], in_=cm[:], pattern=[[1, 128]], compare_op=ALU.is_ge,
                                               fill=0.0, base=0, channel_multiplier=-1),
             reads=["cmaskn"], writes=["cmaskn"])
        sq_, st_, smn = self.seqmask, self.smtmp, self.smaskn
        P.op("pool", lambda e: e.memset(sq_[:], 1.0), writes=["seqmask"])
        P.op("pool", lambda e: e.affine_select(out=sq_[:], in_=sq_[:], pattern=[[-4, 16]], compare_op=ALU.is_ge,
                                               fill=0.0, base=0, channel_multiplier=1),
             reads=["seqmask"], writes=["seqmask"])
        P.op("pool", lambda e: e.affine_select(out=sq_[:], in_=sq_[:], pattern=[[4, 16]], compare_op=ALU.is_ge,
                                               fill=0.0, base=3, channel_multiplier=-1),
             reads=["seqmask"], writes=["seqmask"])
        P.op("pool", lambda e: e.memset(st_[:], -1.0), writes=["smtmp"])
        P.op("pool", lambda e: e.affine_select(out=st_[:], in_=st_[:], pattern=[[-4, 16], [0, 4]],
                                               compare_op=ALU.is_ge, fill=0.0, base=0, channel_multiplier=1),
             reads=["smtmp"], writes=["smtmp"])
        P.op("pool", lambda e: e.affine_select(out=st_[:], in_=st_[:], pattern=[[4, 16], [0, 4]],
                                               compare_op=ALU.is_ge, fill=0.0, base=3, channel_multiplier=-1),
             reads=["smtmp"], writes=["smtmp"])
        P.op("pool", lambda e: e.affine_select(out=st_[:], in_=st_[:], pattern=[[4, 16], [1, 4]],
                                               compare_op=ALU.is_ge, fill=0.0, base=0, channel_multiplier=-1),
             reads=["smtmp"], writes=["smtmp"])
        self.cp("pool", smn[:], st_[:], ["smtmp"], ["smaskn"])
        v = self.vec
        fo = self.fold_out
        P.op("pool", lambda e: e.memset(fo[:], 1.0), writes=["fold_out"])
        for h in range(4):
            self.cp("pool", fo[:, h:h + 1], v[:, 32:33], ["vec", "fold_out"], ["fold_out"])
        lbv = self.lbv
        self.tt("dve", lbv[:, 8:12], v[:, 33:37], v[:, 37:41], ALU.subtract, ["vec"], ["lbv"])
        self.act(lbv[:, 0:4], lbv[:, 8:12], AF.Sigmoid, ["lbv"], ["lbv"])
        self.act(lbv[:, 4:8], lbv[:, 8:12], AF.Sigmoid, ["lbv"], ["lbv"], scale=-1.0)
        self.act(lbv[:, 8:12], lbv[:, 4:8], AF.Ln, ["lbv"], ["lbv"])
        S = self.S
        P.op("pool", lambda e: e.memset(S[:], 0.0), writes=["S0", "S1", "S2", "S3"])
        ub = self.ubuf
        P.op("pool", lambda e: e.memset(ub[:, :, 0:30], 0.0), writes=[("u", c) for c in range(4)])

    def setup_stream(self):
        v = self.vec
        blocks = []

        def kview(w):
            return w.rearrange("(kc p) n -> p kc n", p=128)
        wi = kview(self.w_in)
        for nm, c0 in (("f", 512), ("q", 0), ("og", 1536), ("cg", 2560), ("ca", 2048), ("v", 1024)):
            blocks.append(dict(name="in_" + nm, src=wi[:, :, c0:c0 + 512], KC=8, NB=512, fold=v[:, 0:8]))
        wo = kview(self.w_out)
        for j in range(2):
            blocks.append(dict(name="out%d" % j, src=wo[:, :, j * 512:(j + 1) * 512], KC=8, NB=512,
                               fold=self.fold_out[:, 0:8]))
        wgv, wuv = kview(self.wg), kview(self.wu)
        for j in range(6):
            nb = 512 if j < 5 else 256
            blocks.append(dict(name="g%d" % j, src=wgv[:, :, j * 512:j * 512 + nb], KC=8, NB=nb, fold=v[:, 8:16]))
            blocks.append(dict(name="u%d" % j, src=wuv[:, :, j * 512:j * 512 + nb], KC=8, NB=nb, fold=v[:, 8:16]))
        wdv = kview(self.wd)
        for j in range(4):
            for hf in range(2):
                blocks.append(dict(name="d%d%s" % (j, "ab"[hf]), src=wdv[:, 11 * hf:11 * hf + 11, j * 256:(j + 1) * 256],
                                   KC=11, NB=256, fold=None))
        wpgv = kview(self.wpg)
        for j in range(2):
            blocks.append(dict(name="pg%d" % j, src=wpgv[:, :, j * 512:(j + 1) * 512], KC=8, NB=512,
                               fold=v[:, 16:24]))
        blocks.append(dict(name="pp", src=kview(self.wpp), KC=2, NB=1024, fold=None))
        assert len(blocks) == self.nblocks
        self.blocks = blocks
        self.bidx = {b["name"]: i for i, b in enumerate(blocks)}
        npass = NPASS + (1 if self.do_sample else 0)
        self.stream = [(pi, bi) for pi in range(npass) for bi in range(len(blocks))]
        self.next_load = 0
        self.pending_store = None

    def slot_view(self, sidx, blk):
        s = sidx % RING
        return self.wring[:, s, 0:blk["KC"] * blk["NB"]].rearrange("p (k n) -> p k n", n=blk["NB"])

    def emit_load(self, sidx):
        P = self.P
        pi, bi = self.stream[sidx]
        blk = self.blocks[bi]
        s = sidx % RING
        KC, NB = blk["KC"], blk["NB"]
        n = KC * NB
        slot_key = ("wslot", s)
        convert = (pi == 0) or (pi == 1 and bi % 2 == 1)
        store = (pi == 0 and bi % 2 == 0) or (pi == 1 and bi % 2 == 1)
        if not convert:
            P.dma("sync", self.wring[:, s, 0:n], self.wscr[bi, :, 0:n], reads=[("scr", bi)], writes=[slot_key],
                  key=slot_key)
            if self.pending_store is not None:
                self.flush_store()
            return
        sv = self.slot_view(sidx, blk)
        nparts = min(4, KC)
        bounds = [(KC * q) // nparts for q in range(nparts + 1)]
        for q in range(nparts):
            k0, k1 = bounds[q], bounds[q + 1]
            nk = k1 - k0
            hi = self.alt("wstage", [0, 1, 2, 3])
            stg = self.wstage4[:, hi, 0:nk * NB].rearrange("p (k n) -> p k n", n=NB)
            stkey = ("wstage", hi)
            P.dma("sync", stg, blk["src"][:, k0:k1, :], writes=[stkey], key=stkey)
            if q == 0 and self.pending_store is not None:
                self.flush_store()
            if blk["fold"] is None:
                ceng = "dve" if (blk["name"].startswith("d") and q % 2 == 1) else "pool"
                self.cp(ceng, sv[:, k0:k1, :], stg, [stkey], [slot_key])
            else:
                self.tt("pool", sv[:, k0:k1, :], stg,
                        blk["fold"][:, k0:k1].unsqueeze(2).to_broadcast([128, nk, NB]), ALU.mult,
                        [stkey, "vec", "fold_out"], [slot_key])
        if store:
            self.pending_store = (sidx, bi, n)

    def flush_store(self):
        sidx, bi, n = self.pending_store
        s = sidx % RING
        self.P.dma("sync", self.wscr[bi, :, 0:n], self.wring[:, s, 0:n], reads=[("wslot", s)],
                   writes=[("scr", bi)], key=("wslot", s))
        self.pending_store = None

    def need(self, pi, name, hold=0):
        bi = self.bidx[name]
        sidx = pi * len(self.blocks) + bi
        lim = min(sidx + RING - 1 - hold, len(self.stream) - 1)
        while self.next_load <= lim:
            self.emit_load(self.next_load)
            self.next_load += 1
        blk = self.blocks[bi]
        return self.slot_view(sidx, blk), ("wslot", sidx % RING), blk

    def sumsq_chunk(self, src_ap, src_reads, stbank, stkey, first, last, T):
        i = self.alt("sq", [0, 1, 2])
        sqk = ("sq", i)
        self.act(self.sq[:, i, :T], src_ap, AF.Square, src_reads, [sqk])
        if not hasattr(self, "pstats"):
            self.pstats = []
        self.pstats.append((stbank, stkey, i, sqk, first, last, T))
        while len(self.pstats) > 2:
            self._emit_stat(self.pstats.pop(0))

    def _emit_stat(self, p):
        stbank, stkey, i, sqk, first, last, T = p
        self.mm(stbank[:, :T], [(self.ones_b[:], self.sq[:, i, :T])], [sqk, "ones_b"], [stkey],
                start=first, stop=last)

    def flush_stats(self):
        while getattr(self, "pstats", None):
            self._emit_stat(self.pstats.pop(0))

    def rstd(self, stbank, stkey, n, T, out=None, outkey="rs"):
        out = self.rs if out is None else out
        self.flush_stats()
        self.act(self.rt[:, :T], stbank[:, :T], AF.Ln, [stkey, "epsT"], ["rt"], scale=1.0 / n, bias=self.epsT[:])
        self.act(out[:, :T], self.rt[:, :T], AF.Exp, ["rt"], [outkey], scale=-0.5)

    def make_xn(self, T):
        for c in range(8):
            eng = "dve" if c < 6 else self.pe2
            self.tt(eng, self.xn[:, c, :T], self.hT[:, c, :T], self.rs[:, :T], ALU.mult,
                    [("hT", c), "rs"], [("xn", c)])

    def linear_fm(self, w, wkey, KC, n, rhs_fn, rhs_reads, T, split=False, korder=None):
        bk, bkey = self.bank()
        if split:
            ks = list(range(KC)) if korder is None else korder
            for i, k in enumerate(ks):
                self.mm(bk[:, :T], [(w[:, k, n * 128:(n + 1) * 128], rhs_fn(k))], [wkey, rhs_reads[k]], [bkey],
                        start=(i == 0), stop=(i == KC - 1))
        else:
            pairs = [(w[:, k, n * 128:(n + 1) * 128], rhs_fn(k)) for k in range(KC)]
            self.mm(bk[:, :T], pairs, [wkey] + rhs_reads, [bkey])
        return bk, bkey

    def load_xp(self, pi):
        P = self.P
        sample = pi >= self.npass
        if sample and not self.do_sample:
            return
        NS = 1 if sample else 4
        TT = 64 if sample else 128
        for s in range(NS):
            src = self.xs if sample else self.xp[pi * TP + s * 128: pi * TP + (s + 1) * 128, :]
            P.dma("sync", self.xst[:TT, s, :], src, writes=[("xst", s)], key=("xst", s))
        for s in range(NS):
            src = self.ps_ if sample else self.pp[pi * TP + s * 128: pi * TP + (s + 1) * 128, :]
            P.dma("sync", self.pst[:TT, s, :], src, writes=[("pst", s)], key=("pst", s))

    def run_pass(self, pi, T, sample):
        P = self.P
        self.pe2 = "dve" if (pi <= 1 or sample) else "pool"
        NS = 1 if sample else 4
        TT = 64 if sample else 128
        hT, xn, fb, hid, cat = self.hT, self.xn, self.fb, self.hid, self.cat
        xn_reads = [("xn", c) for c in range(8)]
        xnf = lambda k: xn[:, k, :T]
        QD, NK, VT, OG = 0, 4, 12, 16

        if pi == 0:
            self.load_xp(0)
        if self.stop <= 1:
            return
        self.need(pi, "in_f")

        stb, stk = self.bank()
        for c in range(8):
            bk, bkey = self.bank()
            for s in range(NS):
                self.tr(bk[:, s * 128:s * 128 + TT], self.xst[:TT, s, c * 128:(c + 1) * 128],
                        self.ident_f[:TT, :TT], [("xst", s), "ident_f"], [bkey])
            self.cp("dve", hT[:, c, :T], bk[:, :T], [bkey], [("hT", c)])
            self.sumsq_chunk(bk[:, :T], [bkey], stb, stk, c == 0, c == 7, T)
            self.rel(bkey)
        self.rstd(stb, stk, D, T)
        self.rel(stk)
        self.make_xn(T)
        for c2 in range(2):
            bk, bkey = self.bank()
            for s in range(NS):
                self.tr(bk[:, s * 128:s * 128 + TT], self.pst[:TT, s, c2 * 128:(c2 + 1) * 128],
                        self.ident_f[:TT, :TT], [("pst", s), "ident_f"], [bkey])
            self.cp("act", self.pT[:, c2, :T], bk[:, :T], [bkey], [("pT", c2)])
            self.rel(bkey)
        if not sample:
            self.load_xp(pi + 1)
        else:
            self.load_S0(0)
            self.load_S0(1)

        if self.stop <= 2:
            return
        def tmp(h, j):
            i = (8 + 4 * h + j) if h < 2 else (4 * (h - 2) + j)
            return i, ("fb", i)
        zb = self.yst[:].rearrange("p a (b t) -> p (a b) t", t=512)

        last_prompt = (not sample) and pi == NPASS - 1
        elem = self.hgrn_elem_all(T, sample, tmp)
        w, wkey, blk = self.need(pi, "in_f")
        for h in range(4):
            bk, bkey = self.linear_fm(w, wkey, 8, h, xnf, xn_reads, T, split=(h == 0))
            i, k_ = tmp(h, 0)
            self.act(fb[:, i, :T], bk[:, :T], AF.Sigmoid, [bkey], [k_])
            self.rel(bkey)
        next(elem)
        w, wkey, blk = self.need(pi, "in_q")
        for h in range(4):
            bk, bkey = self.linear_fm(w, wkey, 8, h, xnf, xn_reads, T)
            i, k_ = tmp(h, 3)
            self.act(fb[:, i, :T], bk[:, :T], AF.Silu, [bkey], [k_])
            self.rel(bkey)
        next(elem)
        if self.stop <= 3:
            return
        w, wkey, blk = self.need(pi, "in_og")
        for h in range(4):
            bk, bkey = self.linear_fm(w, wkey, 8, h, xnf, xn_reads, T)
            self.act(hid[:, OG + h, :T], bk[:, :T], AF.Silu, [bkey], [("hid", OG + h)])
            self.rel(bkey)
        w, wkey, blk = self.need(pi, "in_cg")
        for c in range(4):
            bk, bkey = self.linear_fm(w, wkey, 8, c, xnf, xn_reads, T)
            self.act(zb[:, c, :T], bk[:, :T], AF.Sigmoid, [bkey], [("yst", c // 2)])
            self.rel(bkey)
        next(elem)
        w, wkey, blk = self.need(pi, "in_ca")
        for c in range(4):
            bk, bkey = self.linear_fm(w, wkey, 8, c, xnf, xn_reads, T)
            sg_ap, sgk = zb[:, c, :T], ("yst", c // 2)
            if sample:
                self.tt("dve", self.ufs[:, c, :, 30:34], bk[:, :T].rearrange("p (s t) -> p s t", t=4),
                        sg_ap.rearrange("p (s t) -> p s t", t=4), ALU.mult, [bkey, sgk], [("ufs", c)])
                self.tt("dve", self.cst[:, c * 64:(c + 1) * 64], bk[:, :T], sg_ap, ALU.mult,
                        [bkey, sgk], [("cst", c)])
            else:
                self.tt("dve", self.ubuf[:, c, 30:30 + T], bk[:, :T], sg_ap, ALU.mult, [bkey, sgk], [("u", c)])
            if last_prompt:
                self.tt("dve", self.cst[:, c * 32:c * 32 + 30], bk[:, T - 30:T], zb[:, c, T - 30:T], ALU.mult,
                        [bkey, sgk], [("cst", c)])
            self.rel(bkey)
        w, wkey, blk = self.need(pi, "in_v")
        for s in range(NS):
            bk, bkey = self.bank()
            pairs = [(xn[:, k, s * 128:s * 128 + TT], w[:, k, :]) for k in range(8)]
            self.mm(bk[:TT, :], pairs, [wkey] + xn_reads, [bkey])
            self.cp("dve", hid[:TT, VT + s, :], bk[:TT, :], [bkey], [("hid", VT + s)])
            self.rel(bkey)
        next(elem)
        for _ in elem:
            pass

        if last_prompt:
            bk, bkey = self.bank()
            for c in range(4):
                self.tr(bk[:30, c * 128:(c + 1) * 128], self.cst[:, c * 32:c * 32 + 30], self.ident_f[:],
                        [("cst", c), "ident_f"], [bkey])
            self.cp("dve", self.rt[:30, 0:512], bk[:30, :], [bkey], ["rt"])
            self.rel(bkey)
            P.dma("sync", self.scp, self.rt[:30, 0:512], reads=["rt"], key="rt")
        if sample:
            bk, bkey = self.bank()
            for c in range(4):
                self.tr(bk[:64, c * 128:(c + 1) * 128], self.cst[:, c * 64:(c + 1) * 64], self.ident_f[:],
                        [("cst", c), "ident_f"], [bkey])
            self.cp("dve", self.rt[:64, 0:512], bk[:64, :], [bkey], ["rt"])
            self.rel(bkey)
            for i in range(NSEQ):
                P.dma("sync", self.scs[i, 26:30, :], self.rt[4 * i:4 * i + 4, 0:512], reads=["rt"], key="rt")
            P.dma("sync", self.scs[:, 0:26, :], self.sc[:, 4:30, :], key="scs_copy")
            pst2 = self.pst[:].rearrange("p (a b) c -> p a (b c)", a=2)
            for g in range(4):
                gb = g % 2
                pk = [("pst", 2 * gb), ("pst", 2 * gb + 1)]
                P.dma("sync", pst2[:120, gb, :],
                      self.sc[4 * g:4 * g + 4, :, :].rearrange("i j c -> (i j) c"),
                      writes=pk, key=("pst", 2 * gb))
                bk, bkey = self.bank()
                for c in range(4):
                    self.tr(bk[:, c * 120:(c + 1) * 120], pst2[:120, gb, c * 128:(c + 1) * 128],
                            self.ident_f[:120, :120], pk + ["ident_f"], [bkey])
                for c in range(4):
                    self.cp("dve", self.ufs[:, c, 4 * g:4 * g + 4, 0:30],
                            bk[:, c * 120:(c + 1) * 120].rearrange("p (s j) -> p s j", j=30),
                            [bkey], [("ufs", c)])
                self.rel(bkey)

        if self.stop <= 4:
            return
        self.need(pi, "out0")

        def conv_and_evac(c):
            bk, bkey = self.conv_chunk(c, T, sample)
            zk = ("yst", c // 2)
            if c == 0:
                self.cst1 = self.bank()
                self.cst2 = self.bank()
            self.act(zb[:, c, :T], bk[:, :T], AF.Identity, [bkey, "vec"], [zk], bias=self.vec[:, 41 + c:42 + c])
            self.act(cat[:, 4 + c, :T], bk[:, :T], AF.Identity, [bkey, "vec"], [("cat", 4 + c)],
                     bias=self.vec[:, 41 + c:42 + c])
            self.rel(bkey)
            st1, k1 = self.cst1
            st2, k2 = self.cst2
            self.mm(st1[:, :T], [(self.ones_b[:], cat[:, 4 + c, :T])], [("cat", 4 + c), "ones_b"], [k1],
                    start=(c == 0), stop=(c == 3))
            self.sumsq_chunk(zb[:, c, :T], [zk], st2, k2, c == 0, c == 3, T)
        conv_and_evac(0)
        conv_and_evac(1)
        if sample:
            for pi_, pair in enumerate(((0, 1), (2, 3))):
                obs = {h: self.hgrn_mm_sample(h, tmp) for h in pair}
                conv_and_evac(2 + pi_)
                if pi_ == 1:
                    self.conv_ln(T, zb)
                for h in pair:
                    self.hgrn_out(h, obs[h], T)
        else:
            conv_and_evac(2)
            obs = self.hgrn_mm_prompt((0, 1), (lambda: conv_and_evac(3)))
            self.conv_ln(T, zb, "ab")
            for h in (0, 1):
                self.hgrn_out(h, obs[h], T)
            obs = self.hgrn_mm_prompt((2, 3), None)
            self.conv_ln(T, zb, "c")
            for h in (2, 3):
                self.hgrn_out(h, obs[h], T)
        if last_prompt:
            for h in range(4):
                P.dma("act", self.shp[h], self.S[:, h, :], reads=["S%d" % h], key="S%d" % h)

        if self.stop <= 6:
            return
        cat_reads = [("cat", c) for c in range(8)]
        stb, stk = self.bank()
        for j in range(2):
            w, wkey, blk = self.need(pi, "out%d" % j)
            for n in range(4):
                c = 4 * j + n
                bk, bkey = self.linear_fm(w, wkey, 8, n, lambda k: cat[:, k, :T], cat_reads, T, split=(c < 4),
                                          korder=[4, 5, 6, 7, 0, 1, 2, 3])
                self.tt("dve", hT[:, c, :T], hT[:, c, :T], bk[:, :T], ALU.add, [bkey, ("hT", c)], [("hT", c)])
                self.rel(bkey)
                self.sumsq_chunk(hT[:, c, :T], [("hT", c)], stb, stk, c == 0, c == 7, T)
        self.rstd(stb, stk, D, T)
        self.rel(stk)
        self.make_xn(T)

        if self.stop <= 7:
            return
        for j in range(6):
            nch = 4 if j < 5 else 2
            wgt, wgk, _ = self.need(pi, "g%d" % j)
            wut, wuk, _ = self.need(pi, "u%d" % j, hold=1)
            for n in range(nch):
                gbk, gkey = self.linear_fm(wgt, wgk, 8, n, xnf, xn_reads, T, split=(j == 0 and n == 0))
                ub_, ukey = self.linear_fm(wut, wuk, 8, n, xnf, xn_reads, T)
                ti = self.alt("ffn_tmp", [8, 9, 10, 11, 12, 13, 14, 15])
                self.act(fb[:, ti, :T], gbk[:, :T], AF.Silu, [gkey], [("fb", ti)])
                self.rel(gkey)
                self.tt("dve", hid[:, 4 * j + n, :T], fb[:, ti, :T], ub_[:, :T], ALU.mult,
                        [("fb", ti), ukey], [("hid", 4 * j + n)])
                self.rel(ukey)
        hid_reads = [("hid", k) for k in range(22)]
        stb, stk = self.bank()
        for n in range(8):
            j, m = n // 2, n % 2
            if m == 0:
                wa, wak, _ = self.need(pi, "d%da" % j)
                wb, wbk, _ = self.need(pi, "d%db" % j, hold=1)
            bk, bkey = self.bank()
            pairs = [(wa[:, k, m * 128:(m + 1) * 128], hid[:, k, :T]) for k in range(11)] + \
                    [(wb[:, k, m * 128:(m + 1) * 128], hid[:, 11 + k, :T]) for k in range(11)]
            self.mm(bk[:, :T], pairs, [wak, wbk] + hid_reads, [bkey])
            self.tt("dve", hT[:, n, :T], hT[:, n, :T], bk[:, :T], ALU.add, [bkey, ("hT", n)], [("hT", n)])
            self.rel(bkey)
            self.sumsq_chunk(hT[:, n, :T], [("hT", n)], stb, stk, n == 0, n == 7, T)
        self.rstd(stb, stk, D, T)
        self.rel(stk)
        self.make_xn(T)

        if self.stop <= 8:
            return
        for j in range(2):
            w, wkey, blk = self.need(pi, "pg%d" % j)
            for n in range(4):
                c = 4 * j + n
                bk, bkey = self.linear_fm(w, wkey, 8, n, xnf, xn_reads, T, split=(c == 0))
                self.act(fb[:, c, :T], bk[:, :T], AF.Sigmoid, [bkey], [("fb", c)])
                self.rel(bkey)
        w, wkey, blk = self.need(pi, "pp")
        stb, stk = self.bank()
        pT = self.pT
        for c in range(8):
            bk, bkey = self.linear_fm(w, wkey, 2, c, lambda k: pT[:, k, :T], [("pT", 0), ("pT", 1)], T)
            self.tt("dve", fb[:, c, :T], fb[:, c, :T], bk[:, :T], ALU.mult, [bkey, ("fb", c)], [("fb", c)])
            self.rel(bkey)
            self.tt("dve", hT[:, c, :T], hT[:, c, :T], fb[:, c, :T], ALU.add, [("fb", c), ("hT", c)], [("hT", c)])
            self.sumsq_chunk(hT[:, c, :T], [("hT", c)], stb, stk, c == 0, c == 7, T)
        self.rstd(stb, stk, D, T)
        self.rel(stk)
        gfin = self.vec[:, 24:32]
        for c in range(8):
            self.stt(fb[:, 8 + c, :T], hT[:, c, :T], gfin[:, c:c + 1], self.rs[:, :T], ALU.mult, ALU.mult,
                     [("hT", c), "rs", "vec"], [("fb", 8 + c)])
        for s in range(NS):
            yb = self.alt("yst", [0, 1])
            for half in range(2):
                bk, bkey = self.bank()
                for q in range(4):
                    c = half * 4 + q
                    self.tr(bk[:TT, q * 128:(q + 1) * 128], fb[:, 8 + c, s * 128:s * 128 + TT], self.ident_f[:],
                            [("fb", 8 + c), "ident_f"], [bkey])
                eng = self.alt("yev", ["act", "dve"])
                self.cp(eng, self.yst[:TT, yb, half * 512:(half + 1) * 512], bk[:TT, :], [bkey], [("yst", yb)])
                self.rel(bkey)
            dst = self.ys if sample else self.yp[pi * TP + s * 128: pi * TP + (s + 1) * 128, :]
            P.dma("act", dst, self.yst[:TT, yb, :], reads=[("yst", yb)], key=("yst", yb))

    def hgrn_elem_all(self, T, sample, tmp):
        fb, hid, lbv, sm2 = self.fb, self.hid, self.lbv, self.sm2
        QD, NK = 0, 4
        H = range(4)
        sg = {h: tmp(h, 0) for h in H}; lf = {h: tmp(h, 1) for h in H}
        bb = {h: tmp(h, 2) for h in H}; qs = {h: tmp(h, 3) for h in H}
        for h in H:
            self.act(fb[:, lf[h][0], :T], fb[:, sg[h][0], :T], AF.Ln, [sg[h][1], "lbv"], [lf[h][1]],
                     scale=lbv[:, 4 + h:5 + h], bias=lbv[:, h:h + 1])
        yield
        bm, bl = {}, {}
        if not sample:
            for h in H:
                for blk in range(4):
                    sl = slice(blk * 128, (blk + 1) * 128)
                    self.P.op("dve", (lambda o, d1: (lambda e: e.tensor_tensor_scan(
                        out=o, data0=self.ones_f[:], data1=d1, initial=0.0, op0=ALU.mult, op1=ALU.add)))(
                        fb[:, bb[h][0], sl], fb[:, lf[h][0], sl]), [lf[h][1], "ones_f"], [bb[h][1]])
            for h in H:
                bv = fb[:, bb[h][0], :T].rearrange("p (b t) -> p b t", t=128)
                bm[h], bl[h] = bv[:, :, 63], bv[:, :, 127]
                kb, sm2k = bb[h][1], ("sm2", h)
                self.ts("dve", sm2[:, h, 0:4], bm[h], -1.0, None, ALU.mult, ALU.bypass, [kb], [sm2k])
                self.ts("dve", sm2[:, h, 4:8], bm[h], lbv[:, 8 + h:9 + h], None, ALU.add, ALU.bypass, [kb, "lbv"], [sm2k])
                self.tt("dve", sm2[:, h, 20:24], bl[h], bm[h], ALU.subtract, [kb], [sm2k])
        else:
            for h in H:
                kb, klf = bb[h][1], lf[h][1]
                lv = fb[:, lf[h][0], :T].rearrange("p (s t) -> p s t", t=4)
                bv = fb[:, bb[h][0], :T].rearrange("p (s t) -> p s t", t=4)
                self.cp("dve", bv[:, :, 0], lv[:, :, 0], [klf], [kb])
                for t in range(1, 4):
                    self.tt("dve", bv[:, :, t], bv[:, :, t - 1], lv[:, :, t], ALU.add, [klf, kb], [kb])
        yield
        if not sample:
            for h in H:
                kb, sm2k = bb[h][1], ("sm2", h)
                for blk in range(4):
                    sl = slice(blk * 128, (blk + 1) * 128)
                    self.act(fb[:, lf[h][0], sl], fb[:, bb[h][0], sl], AF.Exp, [kb, sm2k], [lf[h][1]],
                             bias=sm2[:, h, blk:blk + 1])
            for h in H:
                kb, sm2k = bb[h][1], ("sm2", h)
                self.act(sm2[:, h, 8:12], bm[h], AF.Exp, [kb], [sm2k])
                self.act(sm2[:, h, 12:16], bl[h], AF.Exp, [kb], [sm2k])
                self.act(sm2[:, h, 16:20], sm2[:, h, 20:24], AF.Exp, [sm2k], [sm2k])
        else:
            for h in H:
                kb, klf, sm2k = bb[h][1], lf[h][1], ("sm2", h)
                lv = fb[:, lf[h][0], :T].rearrange("p (s t) -> p s t", t=4)
                self.act(fb[:, lf[h][0], :T], fb[:, bb[h][0], :T], AF.Exp, [kb], [klf])
                self.cp("dve", sm2[:, h, 0:16], lv[:, :, 3], [klf], [sm2k])
        for h in H:
            self.tt("dve", hid[:, QD + h, :T], fb[:, qs[h][0], :T], fb[:, lf[h][0], :T], ALU.mult,
                    [qs[h][1], lf[h][1]], [("hid", QD + h)])
        yield
        if not sample:
            for h in H:
                kb, sm2k = bb[h][1], ("sm2", h)
                self.ts("dve", sm2[:, h, 16:20], sm2[:, h, 16:20], -1.0, None, ALU.mult, ALU.bypass, [sm2k], [sm2k])
                for blk in range(4):
                    sl = slice(blk * 128, (blk + 1) * 128)
                    self.act(fb[:, bb[h][0], sl], fb[:, bb[h][0], sl], AF.Exp, [kb, sm2k], [kb], scale=-1.0,
                             bias=sm2[:, h, 4 + blk:5 + blk])
        else:
            for h in H:
                kb = bb[h][1]
                self.act(fb[:, bb[h][0], :T], fb[:, bb[h][0], :T], AF.Exp, [kb, "lbv"], [kb], scale=-1.0,
                         bias=lbv[:, 8 + h:9 + h])
        for h in H:
            self.stt(hid[:, NK + h, :T], fb[:, sg[h][0], :T], 1.0, fb[:, bb[h][0], :T], ALU.subtract, ALU.mult,
                     [sg[h][1], bb[h][1]], [("hid", NK + h)])
        yield

    def hgrn_mm_prompt(self, pair, mid_fn=None):
        hid = self.hid
        QD, NK, VT = 0, 4, 12
        sm2 = self.sm2
        obs = {}
        ubanks = {}
        for h in pair:
            hp = h % 2
            qd, nk = hid[:, QD + h, :], hid[:, NK + h, :]
            qk, nkk = ("hid", QD + h), ("hid", NK + h)
            vks = [("hid", VT + blk) for blk in range(4)]
            for blk in range(4):
                sl = slice(blk * 128, (blk + 1) * 128)
                self.tr(self.pkt[:, sl], nk[:, sl], self.ident_b[:], [nkk, "ident_b"], ["pkt"])
            ktkey = ("kt", h)
            self.cp("act", self.ktok[:, :, h * 128:(h + 1) * 128],
                    self.pkt[:, 0:512].rearrange("p (b k) -> p b k", k=128), ["pkt"], [ktkey])
            sb_, sbkey = self.bank()
            for blk in range(4):
                sl = slice(blk * 128, (blk + 1) * 128)
                self.mm(sb_[:, sl], [(nk[:, sl], qd[:, sl])], [nkk, qk], [sbkey])
            scmk = ("scm", hp)
            self.tt("dve", self.scmb[:, 4 * hp:4 * hp + 4, :], sb_[:, :].rearrange("p (b t) -> p b t", t=128),
                    self.cmaskn[:].unsqueeze(1).to_broadcast([128, 4, 128]), ALU.mult,
                    [sbkey, "cmaskn"], [scmk])
            self.rel(sbkey)
            ub_, ubkey = self.bank()
            for blk in range(4):
                sl = slice(blk * 128, (blk + 1) * 128)
                self.mm(ub_[:, sl], [(self.ktok[:, blk, h * 128:(h + 1) * 128],
                                      hid[:, VT + blk, h * 128:(h + 1) * 128])], [ktkey, vks[blk]], [ubkey])
            ubanks[h] = (ub_, ubkey)
        if mid_fn is not None:
            mid_fn()
        for h in pair:
            ub_, ubkey = ubanks[h]
            uv = ub_[:, :].rearrange("p (b v) -> p b v", v=128)
            self.tt("dve", uv, uv, sm2[:, h, 16:20].unsqueeze(2).to_broadcast([128, 4, 128]), ALU.mult,
                    [ubkey, ("sm2", h)], [ubkey])
        for blk in range(4):
            sl = slice(blk * 128, (blk + 1) * 128)
            for h in pair:
                hp = h % 2
                sm2k = ("sm2", h)
                ub_, ubkey = ubanks[h]
                sbk = ("Sb", hp, blk)
                self.act(self.Sbb[:, 4 * hp + blk, :], self.S[:, h, :], AF.Copy, ["S%d" % h, sm2k], [sbk],
                         scale=sm2[:, h, 8 + blk:9 + blk])
                self.stt(self.S[:, h, :], self.S[:, h, :], sm2[:, h, 12 + blk:13 + blk], ub_[:, sl],
                         ALU.mult, ALU.add, ["S%d" % h, sm2k, ubkey], ["S%d" % h])
        for h in pair:
            self.rel(ubanks[h][1])
        for h in pair:
            hp = h % 2
            qd = hid[:, QD + h, :]
            qk = ("hid", QD + h)
            ob, okey = self.bank()
            pairs_reads = [qk, ("scm", hp)]
            def fn(e, ob=ob, h=h, hp=hp, qd=qd):
                ins = None
                for blk in range(4):
                    sl = slice(blk * 128, (blk + 1) * 128)
                    e.matmul(ob[:, sl], lhsT=hid[:, VT + blk, h * 128:(h + 1) * 128],
                             rhs=self.scmb[:, 4 * hp + blk, :], start=True, stop=False)
                    ins = e.matmul(ob[:, sl], lhsT=self.Sbb[:, 4 * hp + blk, :], rhs=qd[:, sl],
                                   start=False, stop=True)
                return ins
            self.P.op("pe", fn, pairs_reads + [("hid", VT + b_) for b_ in range(4)] +
                      [("Sb", hp, b_) for b_ in range(4)], [okey])
            obs[h] = (ob, okey)
        return obs

    def load_S0(self, h):
        j = h % 2
        S0 = self.xst[:, 2 * j:2 * j + 2, :].rearrange("p a (b v) -> p (a b) v", v=128)
        self.P.dma("sync", S0, self.sh[:, h, :, :].rearrange("i k v -> k i v"),
                   writes=[("S0s", j), ("xst", 2 * j), ("xst", 2 * j + 1)], key=("S0s", j))

    def hgrn_mm_sample(self, h, tmp):
        P = self.P
        fb, hid = self.fb, self.hid
        QD, NK, VT = 0, 4, 12
        TT = TS
        (ilf, klf) = tmp(h, 1)
        qd, nk = hid[:, QD + h, :], hid[:, NK + h, :]
        qk, nkk = ("hid", QD + h), ("hid", NK + h)
        ob, okey = self.bank()
        j = h % 2
        S0 = self.xst[:, 2 * j:2 * j + 2, :].rearrange("p a (b v) -> p (a b) v", v=128)
        Sn = self.wstage[:, j, :].rearrange("p (i v) -> p i v", v=128)
        s0k, snk = ("S0s", j), ("wstage", 2 * j)
        x23 = [("wstage", 2 * j + 1)]
        s0bk = [("xn", c) for c in range(4)]
        self.cp("act", self.S0b, S0, [s0k], s0bk)
        ktk = "pkt"
        self.tr(self.pkt[:TT, 0:128], nk[:, :TT], self.ident_b[:], [nkk, "ident_b"], [ktk])
        kt_ap = self.ktok[:TT, 0, h * 128:(h + 1) * 128]
        ktkey = ("kt", h)
        self.cp("dve", kt_ap, self.pkt[:TT, 0:128], [ktk], [ktkey])
        sck = "phg"
        self.mm(self.phg[:TT, 0:TT], [(nk[:, :TT], qd[:, :TT])], [nkk, qk], [sck])
        scm = self.scmb[:TT, 0, 0:TT]
        scmk = ("scm", 0)
        self.tt("dve", scm, self.phg[:TT, 0:TT], self.smaskn[:].rearrange("p a b -> p (a b)"), ALU.mult,
                [sck, "smaskn"], [scmk])
        v_ap = hid[:TT, VT, h * 128:(h + 1) * 128]
        vk = ("hid", VT)
        S0b = self.S0b

        def fn(e, ob=ob, v_ap=v_ap, scm=scm, qd=qd, S0b=S0b):
            ins = e.matmul(ob[:, 0:TS], lhsT=v_ap, rhs=scm, start=True, stop=False)
            for i in range(NSEQ):
                ins = e.matmul(ob[:, 4 * i:4 * i + 4], lhsT=S0b[:, i, :], rhs=qd[:, 4 * i:4 * i + 4],
                               start=False, stop=(i == NSEQ - 1))
            return ins
        P.op("pe", fn, [vk, scmk, qk] + s0bk, [okey])
        for g in range(4):
            self.tt("dve", self.vm[:, 4 * g:4 * g + 4, :], v_ap.unsqueeze(1).to_broadcast([TT, 4, 128]),
                    self.seqmask[:, 4 * g:4 * g + 4].unsqueeze(2).to_broadcast([TT, 4, 128]), ALU.mult,
                    [vk, "seqmask"], [("xn", 4 + g)])
        for g in range(4):
            bk, bkey = self.bank()
            for q in range(4):
                i = 4 * g + q
                self.mm(bk[:, q * 128:(q + 1) * 128], [(kt_ap, self.vm[:, i, :])], [ktkey, ("xn", 4 + i // 4)], [bkey])
            self.stt(Sn[:, 4 * g:4 * g + 4, :], bk[:, :].rearrange("p (q v) -> p q v", v=128), -1.0,
                     S0[:, 4 * g:4 * g + 4, :], ALU.mult, ALU.add, [bkey, s0k], [snk] + x23)
            self.rel(bkey)
            self.tt("dve", Sn[:, 4 * g:4 * g + 4, :], Sn[:, 4 * g:4 * g + 4, :],
                    self.sm2[:, h, 4 * g:4 * g + 4].unsqueeze(2).to_broadcast([128, 4, 128]), ALU.mult,
                    [snk, ("sm2", h)], [snk])
        P.dma("sync", self.shs[:, h, :, :].rearrange("i k v -> k i v"), Sn, reads=[snk] + x23, key=snk)
        if h + 2 < 4:
            self.load_S0(h + 2)
        return ob, okey

    def hgrn_out(self, h, obank, T):
        fb, hid, cat = self.fb, self.hid, self.cat
        OG = 16
        ob, okey = obank
        ok = ("fb", h)
        irt, irs = 12 + 2 * (h % 2), 13 + 2 * (h % 2)
        krt, krs = ("fb", irt), ("fb", irs)
        stb, stk = self.bank()
        self.sumsq_chunk(ob[:, :T], [okey], stb, stk, True, True, T)
        self.flush_stats()
        self.act(fb[:, irt, :T], stb[:, :T], AF.Ln, [stk, "epsT"], [krt], scale=1.0 / 128, bias=self.epsT[:])
        self.act(fb[:, irs, :T], fb[:, irt, :T], AF.Exp, [krt], [krs], scale=-0.5)
        self.rel(stk)
        self.tt("dve", fb[:, h, :T], ob[:, :T], fb[:, irs, :T], ALU.mult, [okey, krs], [ok])
        self.rel(okey)
        self.tt("dve", cat[:, h, :T], fb[:, h, :T], hid[:, OG + h, :T], ALU.mult, [ok, ("hid", OG + h)], [("cat", h)])

    def conv_chunk(self, c, T, sample):
        bk, bkey = self.bank()
        if sample:
            dr = self.ubuf[:].rearrange("p c t -> p (c t)")[:, 0:2048].rearrange("p (s k) -> p s k", k=128)
        else:
            dr = self.ufs[:].rearrange("p c s t -> p (c s t)")[:, 0:2048].rearrange("p (s k) -> p s k", k=128)
        for b0 in range(0, 31, 8):
            nb = min(8, 31 - b0)
            half = self.alt("dr16", [0, 1])
            dk = ("dr16", half)
            out = dr[:, half * 8:half * 8 + nb, :]
            self.tt(self.pe2, out, self.ident_b[:].unsqueeze(1).to_broadcast([128, nb, 128]),
                    self.cw[:, c, b0:b0 + nb].unsqueeze(2).to_broadcast([128, nb, 128]), ALU.mult,
                    ["ident_b", "cw"], [dk])
            for j in range(b0, b0 + nb):
                if sample:
                    rhs = self.ufs[:, c, :, j:j + 4]
                    rk = ("ufs", c)
                else:
                    rhs = self.ubuf[:, c, j:j + T]
                    rk = ("u", c)
                self.mm(bk[:, :T], [(dr[:, half * 8 + j - b0, :], rhs)], [dk, rk], [bkey],
                        start=(j == 0), stop=(j == 30))
        if not sample:
            self.cp(self.pe2, self.ubuf[:, c, 0:30], self.ubuf[:, c, T:T + 30], [("u", c)], [("u", c)])
        return bk, bkey

    def conv_ln(self, T, zb, part="all"):
        fb, cat, v = self.fb, self.cat, self.vec
        msq, msk = fb[:, 8, :T], ("fb", 8)
        var, vk = fb[:, 9, :T], ("fb", 9)
        mr, mrk = fb[:, 10, :T], ("fb", 10)
        if part in ("all", "ab"):
            st1, k1 = self.cst1
            st2, k2 = self.cst2
            self.flush_stats()
            self.act(msq, st1[:, :T], AF.Square, [k1], [msk], scale=1.0 / 512)
            self.stt(var, st2[:, :T], 1.0 / 512, msq, ALU.mult, ALU.subtract, [k2, msk], [vk])
            self.rel(k2)
            self.act(self.rt[:, :T], var, AF.Ln, [vk, "epsT"], ["rt"], bias=self.epsT[:])
            self.act(fb[:, 11, :T], self.rt[:, :T], AF.Exp, ["rt"], [("fb", 11)], scale=-0.5)
            self.stt(mr, st1[:, :T], 1.0 / 512, fb[:, 11, :T], ALU.mult, ALU.mult, [k1, ("fb", 11)], [mrk])
            self.rel(k1)
            for c in range(4):
                zk = ("yst", c // 2)
                self.tt(self.pe2, zb[:, c, :T], zb[:, c, :T], fb[:, 11, :T], ALU.mult, [zk, ("fb", 11)], [zk])
                self.tt(self.pe2, zb[:, c, :T], zb[:, c, :T], mr, ALU.subtract, [zk, mrk], [zk])
        if part in ("all", "c"):
            for c in range(4):
                zk = ("yst", c // 2)
                self.act(cat[:, 4 + c, :T], zb[:, c, :T], AF.Silu, [zk, "vec"], [("cat", 4 + c)],
                         scale=v[:, 45 + c:46 + c], bias=v[:, 49 + c:50 + c])


_CACHE = {}


def _pack_vecs(norm_mix, norm_ffn, norm_ple, norm_final, hgrn_out_norm, lb_logits, conv_dw_bias,
               conv_ln_gain, conv_ln_bias):
    v = np.zeros((128, 64), np.float32)

    def cols(a, n):
        return np.ascontiguousarray(np.asarray(a, np.float32).reshape(n, 128).T)
    v[:, 0:8] = cols(norm_mix[0], 8)
    v[:, 8:16] = cols(norm_ffn[0], 8)
    v[:, 16:24] = cols(norm_ple[0], 8)
    v[:, 24:32] = cols(norm_final, 8)
    v[:, 32] = np.asarray(hgrn_out_norm[0], np.float32)
    v[:, 33:37] = cols(lb_logits[0], 4)
    v[:, 37:41] = cols(lb_logits[1], 4)
    v[:, 41:45] = cols(conv_dw_bias[0], 4)
    v[:, 45:49] = cols(conv_ln_gain[0], 4)
    v[:, 49:53] = cols(conv_ln_bias[0], 4)
    return v


def kernel(x_prompt, x_sample, p_prompt, p_sample, state_hgrn, state_conv, norm_mix, w_in,
           lb_logits, hgrn_out_norm, conv_dw, conv_dw_bias, conv_ln_gain, conv_ln_bias, w_out,
           norm_ffn, w_ffn_gate, w_ffn_up, w_ffn_down, norm_ple, w_ple_gate, w_ple_proj,
           norm_final, _do_sample=True):
    f = lambda a: np.ascontiguousarray(np.asarray(a, dtype=np.float32))
    key = ("nc", _do_sample)
    if key not in _CACHE:
        _CACHE[key] = Builder(_do_sample).build()
    nc = _CACHE[key]
    vecs = _pack_vecs(norm_mix, norm_ffn, norm_ple, norm_final, hgrn_out_norm, lb_logits, conv_dw_bias,
                      conv_ln_gain, conv_ln_bias)
    cw = np.ascontiguousarray(np.asarray(conv_dw[0], np.float32).reshape(31, 4, 128).transpose(2, 1, 0)).reshape(128, 124)
    shared = dict(w_in=f(w_in[0]), w_out=f(w_out[0]), wg=f(w_ffn_gate[0]), wu=f(w_ffn_up[0]), wd=f(w_ffn_down[0]),
                  wpg=f(w_ple_gate[0]), wpp=f(w_ple_proj[0]), vecs=vecs, convw=cw)
    in_maps = []
    for c in range(8):
        m = dict(shared)
        m["xp"] = f(x_prompt[c]); m["pp"] = f(p_prompt[0, c])
        m["xs"] = f(x_sample[16 * c:16 * c + 16]).reshape(TS, D)
        m["ps"] = f(p_sample[0, 16 * c:16 * c + 16]).reshape(TS, 256)
        m["sh"] = f(state_hgrn[0, 16 * c:16 * c + 16]); m["sc"] = f(state_conv[0, 16 * c:16 * c + 16])
        in_maps.append(m)
    res = run_bass_kernel_spmd(nc, in_maps, core_ids=list(range(8)))
    R = res.results
    y_prompt = np.stack([R[c]["yp"] for c in range(8)], 0).astype(np.float32)
    y_sample = np.concatenate([R[c]["ys"].reshape(16, 4, D) for c in range(8)], 0).astype(np.float32)
    shp = np.stack([R[c]["shp"] for c in range(8)], 0)[None].astype(np.float32)
    scp = np.stack([R[c]["scp"] for c in range(8)], 0)[None].astype(np.float32)
    shs = np.concatenate([R[c]["shs"] for c in range(8)], 0)[None].astype(np.float32)
    scs = np.concatenate([R[c]["scs"] for c in range(8)], 0)[None].astype(np.float32)
    return (y_prompt, y_sample, shp, scp, shs, scs)
```

```python
from contextlib import ExitStack
import numpy as np
import concourse.bass as bass
import concourse.mybir as mybir
from concourse.bass_utils import run_bass_kernel_spmd

F32, BF16 = mybir.dt.float32, mybir.dt.bfloat16
AF = mybir.ActivationFunctionType
ALU = mybir.AluOpType
ENGS = ("sync", "act", "dve", "pool", "pe")

D = 1024
DFF = 2816
NSEQ = 16
TS = 64
TP = 512
NPASS = 4
EPS = 1e-6
RING = 5
SLOT = 4096


class Prog:
    def __init__(self):
        self.ops = []
        self.last_w = {}
        self.readers = {}
        self.dma_keys = []

    def _add(self, eng, fn, reads, writes, dma_key=None):
        i = len(self.ops)
        deps = set()
        for r in reads:
            w = self.last_w.get(r)
            if w is not None:
                deps.add(w)
            if r == "phg" or r == "pkt" or (isinstance(r, tuple) and r[0] == "pg"):
                for r2 in self.readers.get(r, ()):
                    if self.ops[r2]["eng"] != eng:
                        deps.add(r2)
        for w_ in writes:
            w = self.last_w.get(w_)
            if w is not None:
                deps.add(w)
            for r in self.readers.get(w_, ()):
                deps.add(r)
        for r in reads:
            self.readers.setdefault(r, []).append(i)
        for w_ in writes:
            self.last_w[w_] = i
            self.readers[w_] = []
        self.ops.append(dict(eng=eng, fn=fn, deps=deps, dma_key=dma_key, idx=i))
        if dma_key is not None and dma_key not in self.dma_keys:
            self.dma_keys.append(dma_key)
        return i

    def op(self, eng, fn, reads=(), writes=()):
        return self._add(eng, fn, tuple(reads), tuple(writes))

    def dma(self, queue, out, in_, reads=(), writes=(), key=None):
        assert queue in ("sync", "act")
        return self._add(queue, lambda e: e.dma_start(out=out, in_=in_), tuple(reads), tuple(writes), key)

    def finalize(self):
        ops = self.ops
        need = [False] * len(ops)
        for o in ops:
            for d in o["deps"]:
                dd = ops[d]
                if dd["dma_key"] is None and dd["eng"] == "pe" and o["eng"] == "pe" and o["dma_key"] is None:
                    continue
                need[d] = True
        eng_cnt = {e: 0 for e in ENGS}
        dma_cnt = {k: 0 for k in self.dma_keys}
        sig = [None] * len(ops)
        for o in ops:
            i = o["idx"]
            if o["dma_key"] is not None:
                dma_cnt[o["dma_key"]] += 1
                sig[i] = ("dma", o["dma_key"], dma_cnt[o["dma_key"]] * 16)
            elif need[i]:
                eng_cnt[o["eng"]] += 1
                sig[i] = ("eng", o["eng"], eng_cnt[o["eng"]])
        cur = {k: 0 for k in self.dma_keys}
        for o in ops:
            waits = {}
            for d in o["deps"]:
                dd = ops[d]
                if dd["dma_key"] is not None:
                    k = ("dma", dd["dma_key"])
                    v = cur[dd["dma_key"]] * 16
                else:
                    if dd["eng"] == "pe" and o["eng"] == "pe" and o["dma_key"] is None:
                        continue
                    k = ("eng", dd["eng"])
                    v = sig[d][2]
                if waits.get(k, 0) < v:
                    waits[k] = v
            o["waits"] = waits
            if o["dma_key"] is not None:
                cur[o["dma_key"]] += 1
        self.final_dma = cur
        self.sig = sig

    def run(self, sems_eng, sems_dma, block):
        ops, sig = self.ops, self.sig
        per = {e: [o for o in ops if o["eng"] == e] for e in ENGS}

        def body(name):
            def f(eng):
                known = {}
                for o in per[name]:
                    for k, v in o["waits"].items():
                        if known.get(k, 0) >= v:
                            continue
                        known[k] = v
                        eng.wait_ge(sems_eng[k[1]] if k[0] == "eng" else sems_dma[k[1]], v)
                    ins = o["fn"](eng)
                    sg = sig[o["idx"]]
                    if sg is not None:
                        if sg[0] == "dma":
                            ins.then_inc(sems_dma[sg[1]], 16)
                        else:
                            ins.then_inc(sems_eng[sg[1]], 1)
                if name == "sync":
                    for k, v in self.final_dma.items():
                        if v > 0:
                            eng.wait_ge(sems_dma[k], v * 16)
            return f

        block.sync(body("sync"))
        block.scalar(body("act"))
        block.vector(body("dve"))
        block.gpsimd(body("pool"))
        block.tensor(body("pe"))


class Builder:
    def __init__(self, do_sample=True, stop=99, npass=NPASS):
        self.do_sample = do_sample
        self.stop = stop
        self.npass = npass
        self.nc = bass.Bass("TRN2", target_bir_lowering=False)
        self.P = Prog()
        self.es = ExitStack()
        self.bank_i = 0
        self.live = [False] * 6
        self.rr = {}

    def sb(self, name, shape, dt):
        return self.es.enter_context(self.nc.sbuf_tensor(name, shape, dt))

    def din(self, name, shape, dt=F32):
        return self.nc.dram_tensor(name, shape, dt, kind="ExternalInput").ap()

    def dout(self, name, shape, dt=F32):
        return self.nc.dram_tensor(name, shape, dt, kind="ExternalOutput").ap()

    def bank(self):
        for _ in range(6):
            i = self.bank_i % 6
            self.bank_i += 1
            if not self.live[i]:
                self.live[i] = True
                return self.pg[i], ("pg", i)
        raise RuntimeError("no free PSUM bank")

    def rel(self, bkey):
        assert self.live[bkey[1]]
        self.live[bkey[1]] = False

    def alt(self, key, choices):
        i = self.rr.get(key, 0)
        self.rr[key] = i + 1
        return choices[i % len(choices)]

    def act(self, out, in_, func, reads, writes, scale=1.0, bias=None):
        kw = dict(out=out, in_=in_, func=func, scale=scale)
        if bias is not None:
            kw["bias"] = bias
        self.P.op("act", lambda e: e.activation(**kw), reads, writes)

    def tt(self, eng, out, in0, in1, op, reads, writes):
        self.P.op(eng, lambda e: e.tensor_tensor(out=out, in0=in0, in1=in1, op=op), reads, writes)

    def ts(self, eng, out, in0, s1, s2, op0, op1, reads, writes):
        self.P.op(eng, lambda e: e.tensor_scalar(out=out, in0=in0, scalar1=s1, scalar2=s2, op0=op0, op1=op1),
                  reads, writes)

    def stt(self, out, in0, scalar, in1, op0, op1, reads, writes):
        self.P.op("dve", lambda e: e.scalar_tensor_tensor(out=out, in0=in0, scalar=scalar, in1=in1,
                                                          op0=op0, op1=op1), reads, writes)

    def cp(self, eng, out, in_, reads, writes):
        if eng == "act":
            self.act(out, in_, AF.Copy, reads, writes)
        else:
            self.P.op(eng, lambda e: e.tensor_copy(out=out, in_=in_), reads, writes)

    def mm(self, out, pairs, reads, writes, start=True, stop=True):
        n = len(pairs)

        def fn(e):
            ins = None
            for i, (l, r) in enumerate(pairs):
                ins = e.matmul(out, lhsT=l, rhs=r, start=(start and i == 0), stop=(stop and i == n - 1))
            return ins
        self.P.op("pe", fn, reads, writes)

    def tr(self, out, in_, ident, reads, writes):
        self.P.op("pe", lambda e: e.transpose(out=out, in_=in_, identity=ident), reads, writes)

    def build(self):
        nc, P = self.nc, self.P
        self.xp = self.din("xp", [2048, D]); self.xs = self.din("xs", [TS, D])
        self.pp = self.din("pp", [2048, 256]); self.ps_ = self.din("ps", [TS, 256])
        self.sh = self.din("sh", [NSEQ, 4, 128, 128]); self.sc = self.din("sc", [NSEQ, 30, 512])
        self.w_in = self.din("w_in", [D, 3072]); self.w_out = self.din("w_out", [D, D])
        self.wg = self.din("wg", [D, DFF]); self.wu = self.din("wu", [D, DFF]); self.wd = self.din("wd", [DFF, D])
        self.wpg = self.din("wpg", [D, D]); self.wpp = self.din("wpp", [256, D])
        self.vecs = self.din("vecs", [128, 64])
        self.convw = self.din("convw", [128, 4 * 31])
        self.yp = self.dout("yp", [2048, D]); self.ys = self.dout("ys", [TS, D])
        self.shp = self.dout("shp", [4, 128, 128]); self.scp = self.dout("scp", [30, 512])
        self.shs = self.dout("shs", [NSEQ, 4, 128, 128]); self.scs = self.dout("scs", [NSEQ, 30, 512])
        self.nblocks = 31
        self.wscr = nc.dram_tensor("wscr", [self.nblocks, 128, SLOT], BF16, kind="Internal").ap()

        sb = self.sb
        self.ident_f = sb("ident_f", [128, 128], F32); self.ident_b = sb("ident_b", [128, 128], BF16)
        self.ones_b = sb("ones_b", [128, 128], BF16); self.ones_f = sb("ones_f", [128, 128], F32)
        self.cmaskn = sb("cmaskn", [128, 128], BF16)
        self.smaskn = sb("smaskn", [64, 16, 4], BF16); self.seqmask = sb("seqmask", [64, 16], F32)
        self.smtmp = sb("smtmp", [64, 16, 4], F32)
        self.epsT = sb("epsT", [128, 1], F32)
        self.vec = sb("vec", [128, 64], F32)
        self.cw = sb("cw", [128, 4, 31], F32)
        self.fold_out = sb("fold_out", [128, 8], F32)
        self.lbv = sb("lbv", [128, 12], F32)
        self.hT = sb("hT", [128, 8, TP], F32)
        self.xn = sb("xn", [128, 8, TP], BF16)
        self.sq = sb("sq", [128, 4, TP], BF16)
        self.rs = sb("rs", [128, TP], F32); self.rt = sb("rt", [128, TP], F32)
        self.hid = sb("hid", [128, 22, TP], BF16)
        self.fb = sb("fb", [128, 16, TP], F32)
        self.cat = sb("cat", [128, 8, TP], BF16)
        self.ubuf = sb("ubuf", [128, 4, 30 + TP], BF16)
        self.pT = sb("pT", [128, 2, TP], BF16)
        self.xst = sb("xst", [128, 4, D], F32)
        self.yst = sb("yst", [128, 2, D], F32)
        self.pst = sb("pst", [128, 4, 256], F32)
        self.S = sb("S", [128, 4, 128], F32)
        self.tmpU = sb("tmpU", [128, 2, 128], F32)
        self.sm = sb("sm", [128, 4, 8], F32)
        self.sm2 = sb("sm2", [128, 4, 24], F32)
        self.ktok = sb("ktok", [128, 4, 512], BF16)
        self.scmb = sb("scmb", [128, 8, 128], BF16)
        self.Sbb = sb("Sbb", [128, 8, 128], BF16)
        self.wring = sb("wring", [128, RING, SLOT], BF16)
        self.wstage = sb("wstage", [128, 2, SLOT // 2], F32)
        self.wstage4 = self.wstage[:].rearrange("p a (b n) -> p (a b) n", n=SLOT // 4)
        self.cst = sb("cst", [128, 512], F32)
        self.ufs = sb("ufs", [128, 4, NSEQ, 34], BF16)
        if self.do_sample:
            self.S0b = self.xn[:, 0:4, :].rearrange("p a (b v) -> p (a b) v", v=128)
            self.vm = self.xn[:64, 4:8, :].rearrange("p a (b v) -> p (a b) v", v=128)
        self.pg = [self.es.enter_context(nc.psum_tensor(f"pg{i}", [128, 512], F32)) for i in range(6)]
        self.phg = self.es.enter_context(nc.psum_tensor("phg", [128, 512], F32))
        self.pkt = self.es.enter_context(nc.psum_tensor("pkt", [128, 1024], BF16))

        self.setup_consts()
        self.setup_stream()
        for pi in range(self.npass):
            self.run_pass(pi, TP, False)
        if self.do_sample:
            self.run_pass(NPASS, TS, True)

        P.finalize()
        sems_eng = {e: self.es.enter_context(nc.semaphore("se_" + e)) for e in ENGS}
        sems_dma = {k: self.es.enter_context(nc.semaphore("sd_%d" % i)) for i, k in enumerate(P.dma_keys)}
        block = self.es.enter_context(nc.Block())
        P.run(sems_eng, sems_dma, block)
        self.es.close()
        return nc

    def setup_consts(self):
        P = self.P
        P.dma("sync", self.vec[:], self.vecs, writes=["vec"], key="c_vec")
        P.dma("sync", self.cw[:].rearrange("p c j -> p (c j)"), self.convw, writes=["cw"], key="c_cw")
        ident_f, ident_b = self.ident_f, self.ident_b
        P.op("pool", lambda e: e.memset(ident_f[:], 1.0), writes=["ident_f"])
        P.op("pool", lambda e: e.affine_select(out=ident_f[:], in_=ident_f[:], pattern=[[-1, 128]],
                                               compare_op=ALU.is_equal, fill=0.0, base=0, channel_multiplier=1),
             reads=["ident_f"], writes=["ident_f"])
        self.cp("pool", ident_b[:], ident_f[:], ["ident_f"], ["ident_b"])
        ones_b, ones_f, epsT = self.ones_b, self.ones_f, self.epsT
        P.op("pool", lambda e: e.memset(ones_b[:], 1.0), writes=["ones_b"])
        P.op("pool", lambda e: e.memset(ones_f[:], 1.0), writes=["ones_f"])
        P.op("pool", lambda e: e.memset(epsT[:], EPS), writes=["epsT"])
        cm = self.cmaskn
        P.op("pool", lambda e: e.memset(cm[:], -1.0), writes=["cmaskn"])
        P.op("pool", lambda e: e.affine_select(out=cm[:], in_=cm[:], pattern=[[1, 128]], compare_op=ALU.is_ge,
                                               fill=0.0, base=0, channel_multiplier=-1),
             reads=["cmaskn"], writes=["cmaskn"])
        sq_, st_, smn = self.seqmask, self.smtmp, self.smaskn
        P.op("pool", lambda e: e.memset(sq_[:], 1.0), writes=["seqmask"])
        P.op("pool", lambda e: e.affine_select(out=sq_[:], in_=sq_[:], pattern=[[-4, 16]], compare_op=ALU.is_ge,
                                               fill=0.0, base=0, channel_multiplier=1),
             reads=["seqmask"], writes=["seqmask"])
        P.op("pool", lambda e: e.affine_select(out=sq_[:], in_=sq_[:], pattern=[[4, 16]], compare_op=ALU.is_ge,
                                               fill=0.0, base=3, channel_multiplier=-1),
             reads=["seqmask"], writes=["seqmask"])
        P.op("pool", lambda e: e.memset(st_[:], -1.0), writes=["smtmp"])
        P.op("pool", lambda e: e.affine_select(out=st_[:], in_=st_[:], pattern=[[-4, 16], [0, 4]],
                                               compare_op=ALU.is_ge, fill=0.0, base=0, channel_multiplier=1),
             reads=["smtmp"], writes=["smtmp"])
        P.op("pool", lambda e: e.affine_select(out=st_[:], in_=st_[:], pattern=[[4, 16], [0, 4]],
                                               compare_op=ALU.is_ge, fill=0.0, base=3, channel_multiplier=-1),
             reads=["smtmp"], writes=["smtmp"])
        P.op("pool", lambda e: e.affine_select(out=st_[:], in_=st_[:], pattern=[[4, 16], [1, 4]],
                                               compare_op=ALU.is_ge, fill=0.0, base=0, channel_multiplier=-1),
             reads=["smtmp"], writes=["smtmp"])
        self.cp("pool", smn[:], st_[:], ["smtmp"], ["smaskn"])
        v = self.vec
        fo = self.fold_out
        P.op("pool", lambda e: e.memset(fo[:], 1.0), writes=["fold_out"])
        for h in range(4):
            self.cp("pool", fo[:, h:h + 1], v[:, 32:33], ["vec", "fold_out"], ["fold_out"])
        lbv = self.lbv
        self.tt("dve", lbv[:, 8:12], v[:, 33:37], v[:, 37:41], ALU.subtract, ["vec"], ["lbv"])
        self.act(lbv[:, 0:4], lbv[:, 8:12], AF.Sigmoid, ["lbv"], ["lbv"])
        self.act(lbv[:, 4:8], lbv[:, 8:12], AF.Sigmoid, ["lbv"], ["lbv"], scale=-1.0)
        self.act(lbv[:, 8:12], lbv[:, 4:8], AF.Ln, ["lbv"], ["lbv"])
        S = self.S
        P.op("pool", lambda e: e.memset(S[:], 0.0), writes=["S0", "S1", "S2", "S3"])
        ub = self.ubuf
        P.op("pool", lambda e: e.memset(ub[:, :, 0:30], 0.0), writes=[("u", c) for c in range(4)])

    def setup_stream(self):
        v = self.vec
        blocks = []

        def kview(w):
            return w.rearrange("(kc p) n -> p kc n", p=128)
        wi = kview(self.w_in)
        for nm, c0 in (("f", 512), ("q", 0), ("og", 1536), ("cg", 2560), ("ca", 2048), ("v", 1024)):
            blocks.append(dict(name="in_" + nm, src=wi[:, :, c0:c0 + 512], KC=8, NB=512, fold=v[:, 0:8]))
        wo = kview(self.w_out)
        for j in range(2):
            blocks.append(dict(name="out%d" % j, src=wo[:, :, j * 512:(j + 1) * 512], KC=8, NB=512,
                               fold=self.fold_out[:, 0:8]))
        wgv, wuv = kview(self.wg), kview(self.wu)
        for j in range(6):
            nb = 512 if j < 5 else 256
            blocks.append(dict(name="g%d" % j, src=wgv[:, :, j * 512:j * 512 + nb], KC=8, NB=nb, fold=v[:, 8:16]))
            blocks.append(dict(name="u%d" % j, src=wuv[:, :, j * 512:j * 512 + nb], KC=8, NB=nb, fold=v[:, 8:16]))
        wdv = kview(self.wd)
        for j in range(4):
            for hf in range(2):
                blocks.append(dict(name="d%d%s" % (j, "ab"[hf]), src=wdv[:, 11 * hf:11 * hf + 11, j * 256:(j + 1) * 256],
                                   KC=11, NB=256, fold=None))
        wpgv = kview(self.wpg)
        for j in range(2):
            blocks.append(dict(name="pg%d" % j, src=wpgv[:, :, j * 512:(j + 1) * 512], KC=8, NB=512,
                               fold=v[:, 16:24]))
        blocks.append(dict(name="pp", src=kview(self.wpp), KC=2, NB=1024, fold=None))
        assert len(blocks) == self.nblocks
        self.blocks = blocks
        self.bidx = {b["name"]: i for i, b in enumerate(blocks)}
        npass = NPASS + (1 if self.do_sample else 0)
        self.stream = [(pi, bi) for pi in range(npass) for bi in range(len(blocks))]
        self.next_load = 0
        self.pending_store = None

    def slot_view(self, sidx, blk):
        s = sidx % RING
        return self.wring[:, s, 0:blk["KC"] * blk["NB"]].rearrange("p (k n) -> p k n", n=blk["NB"])

    def emit_load(self, sidx):
        P = self.P
        pi, bi = self.stream[sidx]
        blk = self.blocks[bi]
        s = sidx % RING
        KC, NB = blk["KC"], blk["NB"]
        n = KC * NB
        slot_key = ("wslot", s)
        convert = (pi == 0) or (pi == 1 and bi % 2 == 1)
        store = (pi == 0 and bi % 2 == 0) or (pi == 1 and bi % 2 == 1)
        if not convert:
            P.dma("sync", self.wring[:, s, 0:n], self.wscr[bi, :, 0:n], reads=[("scr", bi)], writes=[slot_key],
                  key=slot_key)
            if self.pending_store is not None:
                self.flush_store()
            return
        sv = self.slot_view(sidx, blk)
        nparts = min(4, KC)
        bounds = [(KC * q) // nparts for q in range(nparts + 1)]
        for q in range(nparts):
            k0, k1 = bounds[q], bounds[q + 1]
            nk = k1 - k0
            hi = self.alt("wstage", [0, 1, 2, 3])
            stg = self.wstage4[:, hi, 0:nk * NB].rearrange("p (k n) -> p k n", n=NB)
            stkey = ("wstage", hi)
            P.dma("sync", stg, blk["src"][:, k0:k1, :], writes=[stkey], key=stkey)
            if q == 0 and self.pending_store is not None:
                self.flush_store()
            if blk["fold"] is None:
                ceng = "dve" if (blk["name"].startswith("d") and q % 2 == 1) else "pool"
                self.cp(ceng, sv[:, k0:k1, :], stg, [stkey], [slot_key])
            else:
                self.tt("pool", sv[:, k0:k1, :], stg,
                        blk["fold"][:, k0:k1].unsqueeze(2).to_broadcast([128, nk, NB]), ALU.mult,
                        [stkey, "vec", "fold_out"], [slot_key])
        if store:
            self.pending_store = (sidx, bi, n)

    def flush_store(self):
        sidx, bi, n = self.pending_store
        s = sidx % RING
        self.P.dma("sync", self.wscr[bi, :, 0:n], self.wring[:, s, 0:n], reads=[("wslot", s)],
                   writes=[("scr", bi)], key=("wslot", s))
        self.pending_store = None

    def need(self, pi, name, hold=0):
        bi = self.bidx[name]
        sidx = pi * len(self.blocks) + bi
        lim = min(sidx + RING - 1 - hold, len(self.stream) - 1)
        while self.next_load <= lim:
            self.emit_load(self.next_load)
            self.next_load += 1
        blk = self.blocks[bi]
        return self.slot_view(sidx, blk), ("wslot", sidx % RING), blk

    def sumsq_chunk(self, src_ap, src_reads, stbank, stkey, first, last, T):
        i = self.alt("sq", [0, 1, 2, 3])
        sqk = ("sq", i)
        self.act(self.sq[:, i, :T], src_ap, AF.Square, src_reads, [sqk])
        if not hasattr(self, "pstats"):
            self.pstats = []
        self.pstats.append((stbank, stkey, i, sqk, first, last, T))
        while len(self.pstats) > 3:
            self._emit_stat(self.pstats.pop(0))

    def _emit_stat(self, p):
        stbank, stkey, i, sqk, first, last, T = p
        self.mm(stbank[:, :T], [(self.ones_b[:], self.sq[:, i, :T])], [sqk, "ones_b"], [stkey],
                start=first, stop=last)

    def flush_stats(self):
        while getattr(self, "pstats", None):
            self._emit_stat(self.pstats.pop(0))

    def rstd(self, stbank, stkey, n, T, out=None, outkey="rs"):
        out = self.rs if out is None else out
        self.flush_stats()
        self.act(self.rt[:, :T], stbank[:, :T], AF.Ln, [stkey, "epsT"], ["rt"], scale=1.0 / n, bias=self.epsT[:])
        self.act(out[:, :T], self.rt[:, :T], AF.Exp, ["rt"], [outkey], scale=-0.5)

    def make_xn(self, T):
        for c in range(8):
            eng = self.alt("xn", ["dve", self.pe2])
            self.tt(eng, self.xn[:, c, :T], self.hT[:, c, :T], self.rs[:, :T], ALU.mult,
                    [("hT", c), "rs"], [("xn", c)])

    def linear_fm(self, w, wkey, KC, n, rhs_fn, rhs_reads, T, split=False, korder=None):
        bk, bkey = self.bank()
        if split:
            ks = list(range(KC)) if korder is None else korder
            for i, k in enumerate(ks):
                self.mm(bk[:, :T], [(w[:, k, n * 128:(n + 1) * 128], rhs_fn(k))], [wkey, rhs_reads[k]], [bkey],
                        start=(i == 0), stop=(i == KC - 1))
        else:
            pairs = [(w[:, k, n * 128:(n + 1) * 128], rhs_fn(k)) for k in range(KC)]
            self.mm(bk[:, :T], pairs, [wkey] + rhs_reads, [bkey])
        return bk, bkey

    def load_xp(self, pi):
        P = self.P
        sample = pi >= self.npass
        if sample and not self.do_sample:
            return
        NS = 1 if sample else 4
        TT = 64 if sample else 128
        for s in range(NS):
            src = self.xs if sample else self.xp[pi * TP + s * 128: pi * TP + (s + 1) * 128, :]
            P.dma("sync", self.xst[:TT, s, :], src, writes=[("xst", s)], key=("xst", s))
        for s in range(NS):
            src = self.ps_ if sample else self.pp[pi * TP + s * 128: pi * TP + (s + 1) * 128, :]
            P.dma("sync", self.pst[:TT, s, :], src, writes=[("pst", s)], key=("pst", s))

    def run_pass(self, pi, T, sample):
        P = self.P
        self.pe2 = "dve" if (pi <= 1 or sample) else "pool"
        NS = 1 if sample else 4
        TT = 64 if sample else 128
        hT, xn, fb, hid, cat = self.hT, self.xn, self.fb, self.hid, self.cat
        xn_reads = [("xn", c) for c in range(8)]
        xnf = lambda k: xn[:, k, :T]
        QD, NK, VT, OG = 0, 4, 12, 16

        if pi == 0:
            self.load_xp(0)
        if self.stop <= 1:
            return
        self.need(pi, "in_f")

        stb, stk = self.bank()
        for c in range(8):
            bk, bkey = self.bank()
            for s in range(NS):
                self.tr(bk[:, s * 128:s * 128 + TT], self.xst[:TT, s, c * 128:(c + 1) * 128],
                        self.ident_f[:TT, :TT], [("xst", s), "ident_f"], [bkey])
            self.cp("dve", hT[:, c, :T], bk[:, :T], [bkey], [("hT", c)])
            self.sumsq_chunk(bk[:, :T], [bkey], stb, stk, c == 0, c == 7, T)
            self.rel(bkey)
        self.rstd(stb, stk, D, T)
        self.rel(stk)
        self.make_xn(T)
        for c2 in range(2):
            bk, bkey = self.bank()
            for s in range(NS):
                self.tr(bk[:, s * 128:s * 128 + TT], self.pst[:TT, s, c2 * 128:(c2 + 1) * 128],
                        self.ident_f[:TT, :TT], [("pst", s), "ident_f"], [bkey])
            self.cp("act", self.pT[:, c2, :T], bk[:, :T], [bkey], [("pT", c2)])
            self.rel(bkey)
        if not sample:
            self.load_xp(pi + 1)
        else:
            self.load_S0(0)
            self.load_S0(1)

        if self.stop <= 2:
            return
        def tmp(h, j):
            i = (8 + 4 * h + j) if h < 2 else (4 * (h - 2) + j)
            return i, ("fb", i)
        zb = self.yst[:].rearrange("p a (b t) -> p (a b) t", t=512)

        last_prompt = (not sample) and pi == NPASS - 1
        elem = self.hgrn_elem_all(T, sample, tmp)
        w, wkey, blk = self.need(pi, "in_f")
        for h in range(4):
            bk, bkey = self.linear_fm(w, wkey, 8, h, xnf, xn_reads, T, split=(h == 0))
            i, k_ = tmp(h, 0)
            self.act(fb[:, i, :T], bk[:, :T], AF.Sigmoid, [bkey], [k_])
            self.rel(bkey)
        next(elem)
        w, wkey, blk = self.need(pi, "in_q")
        for h in range(4):
            bk, bkey = self.linear_fm(w, wkey, 8, h, xnf, xn_reads, T)
            i, k_ = tmp(h, 3)
            self.act(fb[:, i, :T], bk[:, :T], AF.Silu, [bkey], [k_])
            self.rel(bkey)
        next(elem)
        if self.stop <= 3:
            return
        w, wkey, blk = self.need(pi, "in_og")
        for h in range(4):
            bk, bkey = self.linear_fm(w, wkey, 8, h, xnf, xn_reads, T)
            self.act(hid[:, OG + h, :T], bk[:, :T], AF.Silu, [bkey], [("hid", OG + h)])
            self.rel(bkey)
        w, wkey, blk = self.need(pi, "in_cg")
        for c in range(4):
            bk, bkey = self.linear_fm(w, wkey, 8, c, xnf, xn_reads, T)
            self.act(zb[:, c, :T], bk[:, :T], AF.Sigmoid, [bkey], [("yst", c // 2)])
            self.rel(bkey)
        next(elem)
        w, wkey, blk = self.need(pi, "in_ca")
        for c in range(4):
            bk, bkey = self.linear_fm(w, wkey, 8, c, xnf, xn_reads, T)
            sg_ap, sgk = zb[:, c, :T], ("yst", c // 2)
            if sample:
                self.tt("dve", self.ufs[:, c, :, 30:34], bk[:, :T].rearrange("p (s t) -> p s t", t=4),
                        sg_ap.rearrange("p (s t) -> p s t", t=4), ALU.mult, [bkey, sgk], [("ufs", c)])
                self.tt("dve", self.cst[:, c * 64:(c + 1) * 64], bk[:, :T], sg_ap, ALU.mult,
                        [bkey, sgk], [("cst", c)])
            else:
                self.tt("dve", self.ubuf[:, c, 30:30 + T], bk[:, :T], sg_ap, ALU.mult, [bkey, sgk], [("u", c)])
            if last_prompt:
                self.tt("dve", self.cst[:, c * 32:c * 32 + 30], bk[:, T - 30:T], zb[:, c, T - 30:T], ALU.mult,
                        [bkey, sgk], [("cst", c)])
            self.rel(bkey)
        w, wkey, blk = self.need(pi, "in_v")
        for s in range(NS):
            bk, bkey = self.bank()
            pairs = [(xn[:, k, s * 128:s * 128 + TT], w[:, k, :]) for k in range(8)]
            self.mm(bk[:TT, :], pairs, [wkey] + xn_reads, [bkey])
            self.cp("dve", hid[:TT, VT + s, :], bk[:TT, :], [bkey], [("hid", VT + s)])
            self.rel(bkey)
        next(elem)
        for _ in elem:
            pass

        if last_prompt:
            bk, bkey = self.bank()
            for c in range(4):
                self.tr(bk[:30, c * 128:(c + 1) * 128], self.cst[:, c * 32:c * 32 + 30], self.ident_f[:],
                        [("cst", c), "ident_f"], [bkey])
            self.cp("dve", self.rt[:30, 0:512], bk[:30, :], [bkey], ["rt"])
            self.rel(bkey)
            P.dma("sync", self.scp, self.rt[:30, 0:512], reads=["rt"], key="rt")
        if sample:
            bk, bkey = self.bank()
            for c in range(4):
                self.tr(bk[:64, c * 128:(c + 1) * 128], self.cst[:, c * 64:(c + 1) * 64], self.ident_f[:],
                        [("cst", c), "ident_f"], [bkey])
            self.cp("dve", self.rt[:64, 0:512], bk[:64, :], [bkey], ["rt"])
            self.rel(bkey)
            for i in range(NSEQ):
                P.dma("sync", self.scs[i, 26:30, :], self.rt[4 * i:4 * i + 4, 0:512], reads=["rt"], key="rt")
            P.dma("sync", self.scs[:, 0:26, :], self.sc[:, 4:30, :], key="scs_copy")
            pst2 = self.pst[:].rearrange("p (a b) c -> p a (b c)", a=2)
            for g in range(4):
                gb = g % 2
                pk = [("pst", 2 * gb), ("pst", 2 * gb + 1)]
                P.dma("sync", pst2[:120, gb, :],
                      self.sc[4 * g:4 * g + 4, :, :].rearrange("i j c -> (i j) c"),
                      writes=pk, key=("pst", 2 * gb))
                bk, bkey = self.bank()
                for c in range(4):
                    self.tr(bk[:, c * 120:(c + 1) * 120], pst2[:120, gb, c * 128:(c + 1) * 128],
                            self.ident_f[:120, :120], pk + ["ident_f"], [bkey])
                for c in range(4):
                    self.cp("dve", self.ufs[:, c, 4 * g:4 * g + 4, 0:30],
                            bk[:, c * 120:(c + 1) * 120].rearrange("p (s j) -> p s j", j=30),
                            [bkey], [("ufs", c)])
                self.rel(bkey)

        if self.stop <= 4:
            return
        self.need(pi, "out0")

        def conv_and_evac(c):
            bk, bkey = self.conv_chunk(c, T, sample)
            zk = ("yst", c // 2)
            if c == 0:
                self.cst1 = self.bank()
                self.cst2 = self.bank()
            self.act(zb[:, c, :T], bk[:, :T], AF.Identity, [bkey, "vec"], [zk], bias=self.vec[:, 41 + c:42 + c])
            self.act(cat[:, 4 + c, :T], bk[:, :T], AF.Identity, [bkey, "vec"], [("cat", 4 + c)],
                     bias=self.vec[:, 41 + c:42 + c])
            self.rel(bkey)
            st1, k1 = self.cst1
            st2, k2 = self.cst2
            self.mm(st1[:, :T], [(self.ones_b[:], cat[:, 4 + c, :T])], [("cat", 4 + c), "ones_b"], [k1],
                    start=(c == 0), stop=(c == 3))
            self.sumsq_chunk(zb[:, c, :T], [zk], st2, k2, c == 0, c == 3, T)
        conv_and_evac(0)
        conv_and_evac(1)
        if sample:
            for pi_, pair in enumerate(((0, 1), (2, 3))):
                obs = {h: self.hgrn_mm_sample(h, tmp) for h in pair}
                conv_and_evac(2 + pi_)
                if pi_ == 1:
                    self.conv_ln(T, zb)
                for h in pair:
                    self.hgrn_out(h, obs[h], T)
        else:
            conv_and_evac(2)
            obs = self.hgrn_mm_prompt((0, 1), (lambda: conv_and_evac(3)))
            self.conv_ln(T, zb, "ab")
            for h in (0, 1):
                self.hgrn_out(h, obs[h], T)
            obs = self.hgrn_mm_prompt((2, 3), None)
            self.conv_ln(T, zb, "c")
            for h in (2, 3):
                self.hgrn_out(h, obs[h], T)
        if last_prompt:
            for h in range(4):
                P.dma("act", self.shp[h], self.S[:, h, :], reads=["S%d" % h], key="S%d" % h)

        if self.stop <= 6:
            return
        cat_reads = [("cat", c) for c in range(8)]
        stb, stk = self.bank()
        for j in range(2):
            w, wkey, blk = self.need(pi, "out%d" % j)
            for n in range(4):
                c = 4 * j + n
                bk, bkey = self.linear_fm(w, wkey, 8, n, lambda k: cat[:, k, :T], cat_reads, T, split=(c < 4),
                                          korder=[4, 5, 6, 7, 0, 1, 2, 3])
                self.tt("dve", hT[:, c, :T], hT[:, c, :T], bk[:, :T], ALU.add, [bkey, ("hT", c)], [("hT", c)])
                self.rel(bkey)
                self.sumsq_chunk(hT[:, c, :T], [("hT", c)], stb, stk, c == 0, c == 7, T)
        self.rstd(stb, stk, D, T)
        self.rel(stk)
        self.make_xn(T)

        if self.stop <= 7:
            return
        for j in range(6):
            nch = 4 if j < 5 else 2
            wgt, wgk, _ = self.need(pi, "g%d" % j)
            wut, wuk, _ = self.need(pi, "u%d" % j, hold=1)
            for n in range(nch):
                gbk, gkey = self.linear_fm(wgt, wgk, 8, n, xnf, xn_reads, T, split=(j == 0 and n == 0))
                ub_, ukey = self.linear_fm(wut, wuk, 8, n, xnf, xn_reads, T)
                ti = self.alt("ffn_tmp", [8, 9, 10, 11, 12, 13, 14, 15])
                self.act(fb[:, ti, :T], gbk[:, :T], AF.Silu, [gkey], [("fb", ti)])
                self.rel(gkey)
                self.tt("dve", hid[:, 4 * j + n, :T], fb[:, ti, :T], ub_[:, :T], ALU.mult,
                        [("fb", ti), ukey], [("hid", 4 * j + n)])
                self.rel(ukey)
        hid_reads = [("hid", k) for k in range(22)]
        stb, stk = self.bank()
        for n in range(8):
            j, m = n // 2, n % 2
            if m == 0:
                wa, wak, _ = self.need(pi, "d%da" % j)
                wb, wbk, _ = self.need(pi, "d%db" % j, hold=1)
            bk, bkey = self.bank()
            pairs = [(wa[:, k, m * 128:(m + 1) * 128], hid[:, k, :T]) for k in range(11)] + \
                    [(wb[:, k, m * 128:(m + 1) * 128], hid[:, 11 + k, :T]) for k in range(11)]
            self.mm(bk[:, :T], pairs, [wak, wbk] + hid_reads, [bkey])
            self.tt("dve", hT[:, n, :T], hT[:, n, :T], bk[:, :T], ALU.add, [bkey, ("hT", n)], [("hT", n)])
            self.rel(bkey)
            self.sumsq_chunk(hT[:, n, :T], [("hT", n)], stb, stk, n == 0, n == 7, T)
        self.rstd(stb, stk, D, T)
        self.rel(stk)
        self.make_xn(T)

        if self.stop <= 8:
            return
        for j in range(2):
            w, wkey, blk = self.need(pi, "pg%d" % j)
            for n in range(4):
                c = 4 * j + n
                bk, bkey = self.linear_fm(w, wkey, 8, n, xnf, xn_reads, T, split=(c == 0))
                self.act(fb[:, c, :T], bk[:, :T], AF.Sigmoid, [bkey], [("fb", c)])
                self.rel(bkey)
        w, wkey, blk = self.need(pi, "pp")
        stb, stk = self.bank()
        pT = self.pT
        for c in range(8):
            bk, bkey = self.linear_fm(w, wkey, 2, c, lambda k: pT[:, k, :T], [("pT", 0), ("pT", 1)], T)
            self.tt("dve", fb[:, c, :T], fb[:, c, :T], bk[:, :T], ALU.mult, [bkey, ("fb", c)], [("fb", c)])
            self.rel(bkey)
            self.tt("dve", hT[:, c, :T], hT[:, c, :T], fb[:, c, :T], ALU.add, [("fb", c), ("hT", c)], [("hT", c)])
            self.sumsq_chunk(hT[:, c, :T], [("hT", c)], stb, stk, c == 0, c == 7, T)
        self.rstd(stb, stk, D, T)
        self.rel(stk)
        gfin = self.vec[:, 24:32]
        for c in range(8):
            self.stt(fb[:, 8 + c, :T], hT[:, c, :T], gfin[:, c:c + 1], self.rs[:, :T], ALU.mult, ALU.mult,
                     [("hT", c), "rs", "vec"], [("fb", 8 + c)])
        for s in range(NS):
            yb = self.alt("yst", [0, 1])
            for half in range(2):
                bk, bkey = self.bank()
                for q in range(4):
                    c = half * 4 + q
                    self.tr(bk[:TT, q * 128:(q + 1) * 128], fb[:, 8 + c, s * 128:s * 128 + TT], self.ident_f[:],
                            [("fb", 8 + c), "ident_f"], [bkey])
                eng = self.alt("yev", ["act", "dve"])
                self.cp(eng, self.yst[:TT, yb, half * 512:(half + 1) * 512], bk[:TT, :], [bkey], [("yst", yb)])
                self.rel(bkey)
            dst = self.ys if sample else self.yp[pi * TP + s * 128: pi * TP + (s + 1) * 128, :]
            P.dma("act", dst, self.yst[:TT, yb, :], reads=[("yst", yb)], key=("yst", yb))

    def hgrn_elem_all(self, T, sample, tmp):
        fb, hid, lbv, sm2 = self.fb, self.hid, self.lbv, self.sm2
        QD, NK = 0, 4
        H = range(4)
        sg = {h: tmp(h, 0) for h in H}; lf = {h: tmp(h, 1) for h in H}
        bb = {h: tmp(h, 2) for h in H}; qs = {h: tmp(h, 3) for h in H}
        for h in H:
            self.act(fb[:, lf[h][0], :T], fb[:, sg[h][0], :T], AF.Ln, [sg[h][1], "lbv"], [lf[h][1]],
                     scale=lbv[:, 4 + h:5 + h], bias=lbv[:, h:h + 1])
        yield
        bm, bl = {}, {}
        if not sample:
            for h in H:
                for blk in range(4):
                    sl = slice(blk * 128, (blk + 1) * 128)
                    self.P.op("dve", (lambda o, d1: (lambda e: e.tensor_tensor_scan(
                        out=o, data0=self.ones_f[:], data1=d1, initial=0.0, op0=ALU.mult, op1=ALU.add)))(
                        fb[:, bb[h][0], sl], fb[:, lf[h][0], sl]), [lf[h][1], "ones_f"], [bb[h][1]])
            for h in H:
                bv = fb[:, bb[h][0], :T].rearrange("p (b t) -> p b t", t=128)
                bm[h], bl[h] = bv[:, :, 63], bv[:, :, 127]
                kb, sm2k = bb[h][1], ("sm2", h)
                self.ts("dve", sm2[:, h, 0:4], bm[h], -1.0, None, ALU.mult, ALU.bypass, [kb], [sm2k])
                self.ts("dve", sm2[:, h, 4:8], bm[h], lbv[:, 8 + h:9 + h], None, ALU.add, ALU.bypass, [kb, "lbv"], [sm2k])
                self.tt("dve", sm2[:, h, 20:24], bl[h], bm[h], ALU.subtract, [kb], [sm2k])
        else:
            for h in H:
                kb, klf = bb[h][1], lf[h][1]
                lv = fb[:, lf[h][0], :T].rearrange("p (s t) -> p s t", t=4)
                bv = fb[:, bb[h][0], :T].rearrange("p (s t) -> p s t", t=4)
                self.cp("dve", bv[:, :, 0], lv[:, :, 0], [klf], [kb])
                for t in range(1, 4):
                    self.tt("dve", bv[:, :, t], bv[:, :, t - 1], lv[:, :, t], ALU.add, [klf, kb], [kb])
        yield
        if not sample:
            for h in H:
                kb, sm2k = bb[h][1], ("sm2", h)
                for blk in range(4):
                    sl = slice(blk * 128, (blk + 1) * 128)
                    self.act(fb[:, lf[h][0], sl], fb[:, bb[h][0], sl], AF.Exp, [kb, sm2k], [lf[h][1]],
                             bias=sm2[:, h, blk:blk + 1])
            for h in H:
                kb, sm2k = bb[h][1], ("sm2", h)
                self.act(sm2[:, h, 8:12], bm[h], AF.Exp, [kb], [sm2k])
                self.act(sm2[:, h, 12:16], bl[h], AF.Exp, [kb], [sm2k])
                self.act(sm2[:, h, 16:20], sm2[:, h, 20:24], AF.Exp, [sm2k], [sm2k])
        else:
            for h in H:
                kb, klf, sm2k = bb[h][1], lf[h][1], ("sm2", h)
                lv = fb[:, lf[h][0], :T].rearrange("p (s t) -> p s t", t=4)
                self.act(fb[:, lf[h][0], :T], fb[:, bb[h][0], :T], AF.Exp, [kb], [klf])
                self.cp("dve", sm2[:, h, 0:16], lv[:, :, 3], [klf], [sm2k])
        for h in H:
            self.tt("dve", hid[:, QD + h, :T], fb[:, qs[h][0], :T], fb[:, lf[h][0], :T], ALU.mult,
                    [qs[h][1], lf[h][1]], [("hid", QD + h)])
        yield
        if not sample:
            for h in H:
                kb, sm2k = bb[h][1], ("sm2", h)
                self.ts("dve", sm2[:, h, 16:20], sm2[:, h, 16:20], -1.0, None, ALU.mult, ALU.bypass, [sm2k], [sm2k])
                for blk in range(4):
                    sl = slice(blk * 128, (blk + 1) * 128)
                    self.act(fb[:, bb[h][0], sl], fb[:, bb[h][0], sl], AF.Exp, [kb, sm2k], [kb], scale=-1.0,
                             bias=sm2[:, h, 4 + blk:5 + blk])
        else:
            for h in H:
                kb = bb[h][1]
                self.act(fb[:, bb[h][0], :T], fb[:, bb[h][0], :T], AF.Exp, [kb, "lbv"], [kb], scale=-1.0,
                         bias=lbv[:, 8 + h:9 + h])
        for h in H:
            self.stt(hid[:, NK + h, :T], fb[:, sg[h][0], :T], 1.0, fb[:, bb[h][0], :T], ALU.subtract, ALU.mult,
                     [sg[h][1], bb[h][1]], [("hid", NK + h)])
        yield

    def hgrn_mm_prompt(self, pair, mid_fn=None):
        hid = self.hid
        QD, NK, VT = 0, 4, 12
        sm2 = self.sm2
        obs = {}
        ubanks = {}
        for h in pair:
            hp = h % 2
            qd, nk = hid[:, QD + h, :], hid[:, NK + h, :]
            qk, nkk = ("hid", QD + h), ("hid", NK + h)
            vks = [("hid", VT + blk) for blk in range(4)]
            for blk in range(4):
                sl = slice(blk * 128, (blk + 1) * 128)
                self.tr(self.pkt[:, sl], nk[:, sl], self.ident_b[:], [nkk, "ident_b"], ["pkt"])
            ktkey = ("kt", h)
            self.cp("act", self.ktok[:, :, h * 128:(h + 1) * 128],
                    self.pkt[:, 0:512].rearrange("p (b k) -> p b k", k=128), ["pkt"], [ktkey])
            sb_, sbkey = self.bank()
            for blk in range(4):
                sl = slice(blk * 128, (blk + 1) * 128)
                self.mm(sb_[:, sl], [(nk[:, sl], qd[:, sl])], [nkk, qk], [sbkey])
            scmk = ("scm", hp)
            self.tt("dve", self.scmb[:, 4 * hp:4 * hp + 4, :], sb_[:, :].rearrange("p (b t) -> p b t", t=128),
                    self.cmaskn[:].unsqueeze(1).to_broadcast([128, 4, 128]), ALU.mult,
                    [sbkey, "cmaskn"], [scmk])
            self.rel(sbkey)
            ub_, ubkey = self.bank()
            for blk in range(4):
                sl = slice(blk * 128, (blk + 1) * 128)
                self.mm(ub_[:, sl], [(self.ktok[:, blk, h * 128:(h + 1) * 128],
                                      hid[:, VT + blk, h * 128:(h + 1) * 128])], [ktkey, vks[blk]], [ubkey])
            ubanks[h] = (ub_, ubkey)
        if mid_fn is not None:
            mid_fn()
        for h in pair:
            ub_, ubkey = ubanks[h]
            uv = ub_[:, :].rearrange("p (b v) -> p b v", v=128)
            self.tt("dve", uv, uv, sm2[:, h, 16:20].unsqueeze(2).to_broadcast([128, 4, 128]), ALU.mult,
                    [ubkey, ("sm2", h)], [ubkey])
        for blk in range(4):
            sl = slice(blk * 128, (blk + 1) * 128)
            for h in pair:
                hp = h % 2
                sm2k = ("sm2", h)
                ub_, ubkey = ubanks[h]
                sbk = ("Sb", hp, blk)
                self.act(self.Sbb[:, 4 * hp + blk, :], self.S[:, h, :], AF.Copy, ["S%d" % h, sm2k], [sbk],
                         scale=sm2[:, h, 8 + blk:9 + blk])
                self.stt(self.S[:, h, :], self.S[:, h, :], sm2[:, h, 12 + blk:13 + blk], ub_[:, sl],
                         ALU.mult, ALU.add, ["S%d" % h, sm2k, ubkey], ["S%d" % h])
        for h in pair:
            self.rel(ubanks[h][1])
        for h in pair:
            hp = h % 2
            qd = hid[:, QD + h, :]
            qk = ("hid", QD + h)
            ob, okey = self.bank()
            pairs_reads = [qk, ("scm", hp)]
            def fn(e, ob=ob, h=h, hp=hp, qd=qd):
                ins = None
                for blk in range(4):
                    sl = slice(blk * 128, (blk + 1) * 128)
                    e.matmul(ob[:, sl], lhsT=hid[:, VT + blk, h * 128:(h + 1) * 128],
                             rhs=self.scmb[:, 4 * hp + blk, :], start=True, stop=False)
                    ins = e.matmul(ob[:, sl], lhsT=self.Sbb[:, 4 * hp + blk, :], rhs=qd[:, sl],
                                   start=False, stop=True)
                return ins
            self.P.op("pe", fn, pairs_reads + [("hid", VT + b_) for b_ in range(4)] +
                      [("Sb", hp, b_) for b_ in range(4)], [okey])
            obs[h] = (ob, okey)
        return obs

    def load_S0(self, h):
        j = h % 2
        S0 = self.xst[:, 2 * j:2 * j + 2, :].rearrange("p a (b v) -> p (a b) v", v=128)
        self.P.dma("sync", S0, self.sh[:, h, :, :].rearrange("i k v -> k i v"),
                   writes=[("S0s", j), ("xst", 2 * j), ("xst", 2 * j + 1)], key=("S0s", j))

    def hgrn_mm_sample(self, h, tmp):
        P = self.P
        fb, hid = self.fb, self.hid
        QD, NK, VT = 0, 4, 12
        TT = TS
        (ilf, klf) = tmp(h, 1)
        qd, nk = hid[:, QD + h, :], hid[:, NK + h, :]
        qk, nkk = ("hid", QD + h), ("hid", NK + h)
        ob, okey = self.bank()
        j = h % 2
        S0 = self.xst[:, 2 * j:2 * j + 2, :].rearrange("p a (b v) -> p (a b) v", v=128)
        Sn = self.wstage[:, j, :].rearrange("p (i v) -> p i v", v=128)
        s0k, snk = ("S0s", j), ("wstage", 2 * j)
        x23 = [("wstage", 2 * j + 1)]
        s0bk = [("xn", c) for c in range(4)]
        self.cp("act", self.S0b, S0, [s0k], s0bk)
        ktk = "pkt"
        self.tr(self.pkt[:TT, 0:128], nk[:, :TT], self.ident_b[:], [nkk, "ident_b"], [ktk])
        kt_ap = self.ktok[:TT, 0, h * 128:(h + 1) * 128]
        ktkey = ("kt", h)
        self.cp("dve", kt_ap, self.pkt[:TT, 0:128], [ktk], [ktkey])
        sck = "phg"
        self.mm(self.phg[:TT, 0:TT], [(nk[:, :TT], qd[:, :TT])], [nkk, qk], [sck])
        scm = self.scmb[:TT, 0, 0:TT]
        scmk = ("scm", 0)
        self.tt("dve", scm, self.phg[:TT, 0:TT], self.smaskn[:].rearrange("p a b -> p (a b)"), ALU.mult,
                [sck, "smaskn"], [scmk])
        v_ap = hid[:TT, VT, h * 128:(h + 1) * 128]
        vk = ("hid", VT)
        S0b = self.S0b

        def fn(e, ob=ob, v_ap=v_ap, scm=scm, qd=qd, S0b=S0b):
            ins = e.matmul(ob[:, 0:TS], lhsT=v_ap, rhs=scm, start=True, stop=False)
            for i in range(NSEQ):
                ins = e.matmul(ob[:, 4 * i:4 * i + 4], lhsT=S0b[:, i, :], rhs=qd[:, 4 * i:4 * i + 4],
                               start=False, stop=(i == NSEQ - 1))
            return ins
        P.op("pe", fn, [vk, scmk, qk] + s0bk, [okey])
        for g in range(4):
            self.tt("dve", self.vm[:, 4 * g:4 * g + 4, :], v_ap.unsqueeze(1).to_broadcast([TT, 4, 128]),
                    self.seqmask[:, 4 * g:4 * g + 4].unsqueeze(2).to_broadcast([TT, 4, 128]), ALU.mult,
                    [vk, "seqmask"], [("xn", 4 + g)])
        for g in range(4):
            bk, bkey = self.bank()
            for q in range(4):
                i = 4 * g + q
                self.mm(bk[:, q * 128:(q + 1) * 128], [(kt_ap, self.vm[:, i, :])], [ktkey, ("xn", 4 + i // 4)], [bkey])
            self.stt(Sn[:, 4 * g:4 * g + 4, :], bk[:, :].rearrange("p (q v) -> p q v", v=128), -1.0,
                     S0[:, 4 * g:4 * g + 4, :], ALU.mult, ALU.add, [bkey, s0k], [snk] + x23)
            self.rel(bkey)
            self.tt("dve", Sn[:, 4 * g:4 * g + 4, :], Sn[:, 4 * g:4 * g + 4, :],
                    self.sm2[:, h, 4 * g:4 * g + 4].unsqueeze(2).to_broadcast([128, 4, 128]), ALU.mult,
                    [snk, ("sm2", h)], [snk])
        P.dma("sync", self.shs[:, h, :, :].rearrange("i k v -> k i v"), Sn, reads=[snk] + x23, key=snk)
        if h + 2 < 4:
            self.load_S0(h + 2)
        return ob, okey

    def hgrn_out(self, h, obank, T):
        fb, hid, cat = self.fb, self.hid, self.cat
        OG = 16
        ob, okey = obank
        ok = ("fb", h)
        irt, irs = 12 + 2 * (h % 2), 13 + 2 * (h % 2)
        krt, krs = ("fb", irt), ("fb", irs)
        stb, stk = self.bank()
        self.sumsq_chunk(ob[:, :T], [okey], stb, stk, True, True, T)
        self.flush_stats()
        self.act(fb[:, irt, :T], stb[:, :T], AF.Ln, [stk, "epsT"], [krt], scale=1.0 / 128, bias=self.epsT[:])
        self.act(fb[:, irs, :T], fb[:, irt, :T], AF.Exp, [krt], [krs], scale=-0.5)
        self.rel(stk)
        self.tt("dve", fb[:, h, :T], ob[:, :T], fb[:, irs, :T], ALU.mult, [okey, krs], [ok])
        self.rel(okey)
        self.tt("dve", cat[:, h, :T], fb[:, h, :T], hid[:, OG + h, :T], ALU.mult, [ok, ("hid", OG + h)], [("cat", h)])

    def conv_chunk(self, c, T, sample):
        bk, bkey = self.bank()
        if sample:
            dr = self.ubuf[:].rearrange("p c t -> p (c t)")[:, 0:2048].rearrange("p (s k) -> p s k", k=128)
        else:
            dr = self.ufs[:].rearrange("p c s t -> p (c s t)")[:, 0:2048].rearrange("p (s k) -> p s k", k=128)
        for b0 in range(0, 31, 8):
            nb = min(8, 31 - b0)
            half = self.alt("dr16", [0, 1])
            dk = ("dr16", half)
            out = dr[:, half * 8:half * 8 + nb, :]
            self.tt(self.pe2, out, self.ident_b[:].unsqueeze(1).to_broadcast([128, nb, 128]),
                    self.cw[:, c, b0:b0 + nb].unsqueeze(2).to_broadcast([128, nb, 128]), ALU.mult,
                    ["ident_b", "cw"], [dk])
            for j in range(b0, b0 + nb):
                if sample:
                    rhs = self.ufs[:, c, :, j:j + 4]
                    rk = ("ufs", c)
                else:
                    rhs = self.ubuf[:, c, j:j + T]
                    rk = ("u", c)
                self.mm(bk[:, :T], [(dr[:, half * 8 + j - b0, :], rhs)], [dk, rk], [bkey],
                        start=(j == 0), stop=(j == 30))
        if not sample:
            self.cp(self.pe2, self.ubuf[:, c, 0:30], self.ubuf[:, c, T:T + 30], [("u", c)], [("u", c)])
        return bk, bkey

    def conv_ln(self, T, zb, part="all"):
        fb, cat, v = self.fb, self.cat, self.vec
        msq, msk = fb[:, 8, :T], ("fb", 8)
        var, vk = fb[:, 9, :T], ("fb", 9)
        mr, mrk = fb[:, 10, :T], ("fb", 10)
        if part in ("all", "ab"):
            st1, k1 = self.cst1
            st2, k2 = self.cst2
            self.flush_stats()
            self.act(msq, st1[:, :T], AF.Square, [k1], [msk], scale=1.0 / 512)
            self.stt(var, st2[:, :T], 1.0 / 512, msq, ALU.mult, ALU.subtract, [k2, msk], [vk])
            self.rel(k2)
            self.act(self.rt[:, :T], var, AF.Ln, [vk, "epsT"], ["rt"], bias=self.epsT[:])
            self.act(fb[:, 11, :T], self.rt[:, :T], AF.Exp, ["rt"], [("fb", 11)], scale=-0.5)
            self.stt(mr, st1[:, :T], 1.0 / 512, fb[:, 11, :T], ALU.mult, ALU.mult, [k1, ("fb", 11)], [mrk])
            self.rel(k1)
            for c in range(4):
                zk = ("yst", c // 2)
                self.tt(self.pe2, zb[:, c, :T], zb[:, c, :T], fb[:, 11, :T], ALU.mult, [zk, ("fb", 11)], [zk])
                self.tt(self.pe2, zb[:, c, :T], zb[:, c, :T], mr, ALU.subtract, [zk, mrk], [zk])
        if part in ("all", "c"):
            for c in range(4):
                zk = ("yst", c // 2)
                self.act(cat[:, 4 + c, :T], zb[:, c, :T], AF.Silu, [zk, "vec"], [("cat", 4 + c)],
                         scale=v[:, 45 + c:46 + c], bias=v[:, 49 + c:50 + c])


_CACHE = {}


def _pack_vecs(norm_mix, norm_ffn, norm_ple, norm_final, hgrn_out_norm, lb_logits, conv_dw_bias,
               conv_ln_gain, conv_ln_bias):
    v = np.zeros((128, 64), np.float32)

    def cols(a, n):
        return np.ascontiguousarray(np.asarray(a, np.float32).reshape(n, 128).T)
    v[:, 0:8] = cols(norm_mix[0], 8)
    v[:, 8:16] = cols(norm_ffn[0], 8)
    v[:, 16:24] = cols(norm_ple[0], 8)
    v[:, 24:32] = cols(norm_final, 8)
    v[:, 32] = np.asarray(hgrn_out_norm[0], np.float32)
    v[:, 33:37] = cols(lb_logits[0], 4)
    v[:, 37:41] = cols(lb_logits[1], 4)
    v[:, 41:45] = cols(conv_dw_bias[0], 4)
    v[:, 45:49] = cols(conv_ln_gain[0], 4)
    v[:, 49:53] = cols(conv_ln_bias[0], 4)
    return v


def kernel(x_prompt, x_sample, p_prompt, p_sample, state_hgrn, state_conv, norm_mix, w_in,
           lb_logits, hgrn_out_norm, conv_dw, conv_dw_bias, conv_ln_gain, conv_ln_bias, w_out,
           norm_ffn, w_ffn_gate, w_ffn_up, w_ffn_down, norm_ple, w_ple_gate, w_ple_proj,
           norm_final, _do_sample=True):
    f = lambda a: np.ascontiguousarray(np.asarray(a, dtype=np.float32))
    key = ("nc", _do_sample)
    if key not in _CACHE:
        _CACHE[key] = Builder(_do_sample).build()
    nc = _CACHE[key]
    vecs = _pack_vecs(norm_mix, norm_ffn, norm_ple, norm_final, hgrn_out_norm, lb_logits, conv_dw_bias,
                      conv_ln_gain, conv_ln_bias)
    cw = np.ascontiguousarray(np.asarray(conv_dw[0], np.float32).reshape(31, 4, 128).transpose(2, 1, 0)).reshape(128, 124)
    shared = dict(w_in=f(w_in[0]), w_out=f(w_out[0]), wg=f(w_ffn_gate[0]), wu=f(w_ffn_up[0]), wd=f(w_ffn_down[0]),
                  wpg=f(w_ple_gate[0]), wpp=f(w_ple_proj[0]), vecs=vecs, convw=cw)
    in_maps = []
    for c in range(8):
        m = dict(shared)
        m["xp"] = f(x_prompt[c]); m["pp"] = f(p_prompt[0, c])
        m["xs"] = f(x_sample[16 * c:16 * c + 16]).reshape(TS, D)
        m["ps"] = f(p_sample[0, 16 * c:16 * c + 16]).reshape(TS, 256)
        m["sh"] = f(state_hgrn[0, 16 * c:16 * c + 16]); m["sc"] = f(state_conv[0, 16 * c:16 * c + 16])
        in_maps.append(m)
    res = run_bass_kernel_spmd(nc, in_maps, core_ids=list(range(8)))
    R = res.results
    y_prompt = np.stack([R[c]["yp"] for c in range(8)], 0).astype(np.float32)
    y_sample = np.concatenate([R[c]["ys"].reshape(16, 4, D) for c in range(8)], 0).astype(np.float32)
    shp = np.stack([R[c]["shp"] for c in range(8)], 0)[None].astype(np.float32)
    scp = np.stack([R[c]["scp"] for c in range(8)], 0)[None].astype(np.float32)
    shs = np.concatenate([R[c]["shs"] for c in range(8)], 0)[None].astype(np.float32)
    scs = np.concatenate([R[c]["scs"] for c in range(8)], 0)[None].astype(np.float32)
    return (y_prompt, y_sample, shp, scp, shs, scs)
```
